# Optimizing a Trainium2 kernel written in Bass

```python
import math
import numpy as np
import jax, jax.numpy as jnp
from jax import lax

D_MODEL = 1024
BATCH = 8
SEQ = 4096
DEPTH = 4

DN_HEADS = 4
DN_HEAD_DIM = 128
DN_WIDTH = DN_HEADS * DN_HEAD_DIM
DN_CONV = 4
DN_CHUNK = 64
SWA_HEADS = 8
SWA_HEAD_DIM = 64
SWA_WIDTH = SWA_HEADS * SWA_HEAD_DIM
DILATED_CONFIGS = ((128, 1), (512, 4), (2048, 16))
ATTN_BLOCK = 128
ROPE_THETA = 500000.0
ROPE_DIM = SWA_HEAD_DIM // 4
D_MIX = DN_WIDTH + SWA_WIDTH
IN_SPLITS = (3 * DN_WIDTH, DN_WIDTH, DN_HEADS, DN_HEADS, SWA_WIDTH, SWA_WIDTH, SWA_WIDTH)
IN_WIDTH = sum(IN_SPLITS)
D_FF = 2816
FFN_CONV = 3
NORM_EPS = 1e-6

kernel_name = "hymba_style_deltanet_dilated_swa_convglu"


def _rmsnorm(x, w):
    xf = x.astype(jnp.float32)
    y = xf * lax.rsqrt(jnp.mean(xf * xf, axis=-1, keepdims=True) + NORM_EPS)
    return (y * w.astype(jnp.float32)).astype(x.dtype)


def _l2norm(x):
    xf = x.astype(jnp.float32)
    return xf * lax.rsqrt(jnp.sum(xf * xf, axis=-1, keepdims=True) + NORM_EPS)


def _causal_dwconv(x, w):
    K = w.shape[0]
    S = x.shape[1]
    xp = jnp.pad(x, ((0, 0), (K - 1, 0), (0, 0)))
    y = xp[:, 0:S] * w[0]
    for j in range(1, K):
        y = y + xp[:, j:j + S] * w[j]
    return y


def _rope_tables(S):
    pos = jnp.arange(S, dtype=jnp.float32)
    inv_freq = ROPE_THETA ** (-jnp.arange(0, ROPE_DIM, 2, dtype=jnp.float32) / ROPE_DIM)
    ang = pos[:, None] * inv_freq[None, :]
    return jnp.cos(ang), jnp.sin(ang)


def _partial_rope(x, cos, sin):
    half = ROPE_DIM // 2
    c = cos[None, :, None, :].astype(x.dtype)
    s = sin[None, :, None, :].astype(x.dtype)
    x1, x2, rest = x[..., :half], x[..., half:ROPE_DIM], x[..., ROPE_DIM:]
    return jnp.concatenate([x1 * c - x2 * s, x2 * c + x1 * s, rest], axis=-1)


def _gated_delta_rule(q, k, v, g, beta):
    B, S, H, Dk = q.shape
    Dv = v.shape[-1]
    C = DN_CHUNK
    n = S // C
    f32 = jnp.float32

    def chunks(t):
        return t.astype(f32).reshape(B, n, C, H, -1).transpose(0, 3, 1, 2, 4)

    q = chunks(q) * (Dk ** -0.5)
    k = chunks(k)
    v = chunks(v)
    g = g.astype(f32).reshape(B, n, C, H).transpose(0, 3, 1, 2)
    beta = beta.astype(f32).reshape(B, n, C, H).transpose(0, 3, 1, 2)
    G = jnp.cumsum(g, axis=-1)
    idx = jnp.arange(C)
    incl = idx[:, None] >= idx[None, :]
    strict = idx[:, None] > idx[None, :]
    gamma = jnp.exp(jnp.where(incl, G[..., :, None] - G[..., None, :], -jnp.inf))
    kb = k * beta[..., None]
    a_kk = jnp.where(strict, jnp.einsum('bhnid,bhnjd->bhnij', kb, k) * gamma, 0.0)
    tri = a_kk + jnp.eye(C, dtype=f32)
    rhs = jnp.concatenate([v * beta[..., None], kb * jnp.exp(G)[..., None]], axis=-1)
    sol = lax.linalg.triangular_solve(tri, rhs, left_side=True, lower=True, unit_diagonal=True)
    u, w = sol[..., :Dv], sol[..., Dv:]
    qk = jnp.einsum('bhnid,bhnjd->bhnij', q, k) * gamma
    qg = q * jnp.exp(G)[..., None]
    kd = k * jnp.exp(G[..., -1:] - G)[..., None]
    dc = jnp.exp(G[..., -1])

    def step(state, xs):
        u_c, w_c, qk_c, qg_c, kd_c, dc_c = xs
        v_new = u_c - jnp.einsum('bhcd,bhde->bhce', w_c, state)
        o_c = jnp.einsum('bhcd,bhde->bhce', qg_c, state) + jnp.einsum('bhij,bhje->bhie', qk_c, v_new)
        state = state * dc_c[..., None, None] + jnp.einsum('bhcd,bhce->bhde', kd_c, v_new)
        return state, o_c

    xs = tuple(jnp.moveaxis(t, 2, 0) for t in (u, w, qk, qg, kd, dc))
    state0 = jnp.zeros((B, H, Dk, Dv), f32)
    _, o = lax.scan(step, state0, xs)
    return o.transpose(1, 0, 3, 2, 4).reshape(B, S, H, Dv)


def _causal_window_attention(q, k, v, span):
    N, L, H, Dh = q.shape
    nb = -(-L // ATTN_BLOCK)
    pad = nb * ATTN_BLOCK - L
    qb = jnp.pad(q, ((0, 0), (0, pad), (0, 0), (0, 0))).reshape(N, nb, ATTN_BLOCK, H, Dh)

    def kv_blocks(t):
        t = jnp.pad(t, ((0, 0), (ATTN_BLOCK, pad), (0, 0), (0, 0))).reshape(N, nb + 1, ATTN_BLOCK, H, Dh)
        return jnp.concatenate([t[:, :-1], t[:, 1:]], axis=2)

    kw, vw = kv_blocks(k), kv_blocks(v)
    s = jnp.einsum('nbqhd,nbkhd->nbhqk', qb, kw, preferred_element_type=jnp.float32)
    qi = jnp.arange(ATTN_BLOCK)[:, None]
    ki = jnp.arange(2 * ATTN_BLOCK)[None, :]
    dist = qi + ATTN_BLOCK - ki
    key_pos = jnp.arange(nb)[:, None, None] * ATTN_BLOCK - ATTN_BLOCK + ki[None]
    valid = (dist >= 0) & (dist <= span) & (key_pos >= 0)
    s = jnp.where(valid[None, :, None], s, -jnp.inf)
    lse = jax.nn.logsumexp(s, axis=-1)
    p = jnp.exp(s - lse[..., None])
    o = jnp.einsum('nbhqk,nbkhd->nbqhd', p, vw.astype(jnp.float32))
    o = o.reshape(N, nb * ATTN_BLOCK, H, Dh)[:, :L]
    lse = lse.transpose(0, 1, 3, 2).reshape(N, nb * ATTN_BLOCK, H)[:, :L]
    return o, lse


def _dilated_attention(q, k, v):
    B, S, H, Dh = q.shape
    q = q * (Dh ** -0.5)
    outs, lses = [], []
    for window, dil in DILATED_CONFIGS:
        L = S // dil

        def by_residue(t):
            return t.reshape(B, L, dil, H, Dh).transpose(0, 2, 1, 3, 4).reshape(B * dil, L, H, Dh)

        o, lse = _causal_window_attention(by_residue(q), by_residue(k), by_residue(v), window // dil)
        outs.append(o.reshape(B, dil, L, H, Dh).transpose(0, 2, 1, 3, 4).reshape(B, S, H, Dh))
        lses.append(lse.reshape(B, dil, L, H).transpose(0, 2, 1, 3).reshape(B, S, H))
    wts = jax.nn.softmax(jnp.stack(lses), axis=0)
    o = jnp.einsum('gbsh,gbshd->bshd', wts, jnp.stack(outs))
    return o.astype(v.dtype)


def _hybrid_mixer(h, cos, sin, w_in, dn_conv, dn_a_log, dn_dt_bias, dn_out_norm, w_out):
    B, S, _ = h.shape
    proj = h @ w_in
    a_qkv, a_z, a_b, a_a, b_q, b_k, b_v = jnp.split(proj, np.cumsum(IN_SPLITS)[:-1], axis=-1)
    a_qkv = jax.nn.silu(_causal_dwconv(a_qkv, dn_conv))
    aq, ak, av = jnp.split(a_qkv, 3, axis=-1)
    heads_a = lambda t: t.reshape(B, S, DN_HEADS, DN_HEAD_DIM)
    aq, ak, av = _l2norm(heads_a(aq)), _l2norm(heads_a(ak)), heads_a(av)
    beta = jax.nn.sigmoid(a_b.astype(jnp.float32))
    g = -jnp.exp(dn_a_log.astype(jnp.float32)) * jax.nn.softplus(a_a.astype(jnp.float32) + dn_dt_bias.astype(jnp.float32))
    o_a = _gated_delta_rule(aq, ak, av, g, beta).astype(h.dtype)
    o_a = _rmsnorm(o_a, dn_out_norm) * jax.nn.silu(heads_a(a_z))
    heads_b = lambda t: t.reshape(B, S, SWA_HEADS, SWA_HEAD_DIM)
    bq = _partial_rope(heads_b(b_q), cos, sin)
    bk = _partial_rope(heads_b(b_k), cos, sin)
    o_b = _dilated_attention(bq, bk, heads_b(b_v))
    mixed = jnp.concatenate([o_a.reshape(B, S, DN_WIDTH), o_b.reshape(B, S, SWA_WIDTH)], axis=-1)
    return mixed @ w_out


def _conv_glu_ffn(h, ffn_up, ffn_conv, ffn_conv_bias, ffn_down):
    u = _causal_dwconv(h @ ffn_up, ffn_conv) + ffn_conv_bias
    gate, val = jnp.split(u, 2, axis=-1)
    return (jax.nn.gelu(gate, approximate=True) * val) @ ffn_down


def setup_inputs(seed: int = 0) -> dict:
    key = jax.random.key(seed)
    ks = jax.random.split(key, 16)
    f32 = jnp.float32

    def normal(k, shape, scale):
        return jax.random.normal(k, shape, f32) * scale

    def gain(k, shape):
        return 1.0 + 0.02 * jax.random.normal(k, shape, f32)

    x = normal(ks[0], (BATCH, SEQ, D_MODEL), 1.0)
    pre_mix_norm = gain(ks[1], (DEPTH, D_MODEL))
    w_in = normal(ks[2], (DEPTH, D_MODEL, IN_WIDTH), D_MODEL ** -0.5)
    dn_conv = normal(ks[3], (DEPTH, DN_CONV, 3 * DN_WIDTH), DN_CONV ** -0.5)
    dn_a_log = jnp.log(jax.random.uniform(ks[4], (DEPTH, DN_HEADS), f32, 1.0, 16.0))
    dt = jnp.exp(jax.random.uniform(ks[5], (DEPTH, DN_HEADS), f32, math.log(1e-3), math.log(1e-1)))
    dn_dt_bias = dt + jnp.log(-jnp.expm1(-dt))
    dn_out_norm = gain(ks[6], (DEPTH, DN_HEAD_DIM))
    w_out = normal(ks[7], (DEPTH, D_MIX, D_MODEL), D_MIX ** -0.5)
    post_mix_norm = gain(ks[8], (DEPTH, D_MODEL))
    pre_ffn_norm = gain(ks[9], (DEPTH, D_MODEL))
    ffn_up = normal(ks[10], (DEPTH, D_MODEL, 2 * D_FF), D_MODEL ** -0.5)
    ffn_conv = normal(ks[11], (DEPTH, FFN_CONV, 2 * D_FF), FFN_CONV ** -0.5)
    ffn_conv_bias = normal(ks[12], (DEPTH, 2 * D_FF), 0.02)
    ffn_down = normal(ks[13], (DEPTH, D_FF, D_MODEL), D_FF ** -0.5)
    post_ffn_norm = gain(ks[14], (DEPTH, D_MODEL))
    return {"x": x, "pre_mix_norm": pre_mix_norm, "w_in": w_in, "dn_conv": dn_conv,
            "dn_a_log": dn_a_log, "dn_dt_bias": dn_dt_bias, "dn_out_norm": dn_out_norm,
            "w_out": w_out, "post_mix_norm": post_mix_norm, "pre_ffn_norm": pre_ffn_norm,
            "ffn_up": ffn_up, "ffn_conv": ffn_conv, "ffn_conv_bias": ffn_conv_bias,
            "ffn_down": ffn_down, "post_ffn_norm": post_ffn_norm}


def reference(x, pre_mix_norm, w_in, dn_conv, dn_a_log, dn_dt_bias, dn_out_norm, w_out,
              post_mix_norm, pre_ffn_norm, ffn_up, ffn_conv, ffn_conv_bias, ffn_down, post_ffn_norm):
    S = x.shape[1]
    cos, sin = _rope_tables(S)
    for l in range(DEPTH):
        h = _rmsnorm(x, pre_mix_norm[l])
        h = _hybrid_mixer(h, cos, sin, w_in[l], dn_conv[l], dn_a_log[l], dn_dt_bias[l], dn_out_norm[l], w_out[l])
        x = x + _rmsnorm(h, post_mix_norm[l])
        h = _rmsnorm(x, pre_ffn_norm[l])
        h = _conv_glu_ffn(h, ffn_up[l], ffn_conv[l], ffn_conv_bias[l], ffn_down[l])
        x = x + _rmsnorm(h, post_ffn_norm[l])
    return x
```

```python
import numpy as np
import ml_dtypes
from contextlib import ExitStack
import concourse.bass as bass
import concourse.mybir as mybir
from concourse.bass_utils import run_bass_kernel_spmd

F32 = mybir.dt.float32
BF16 = mybir.dt.bfloat16
AF = mybir.ActivationFunctionType
ALU = mybir.AluOpType
AX = mybir.AxisListType

SAME_ENGINE_SYNC = True
MULTI_CORE = True
D_MODEL = 1024
DEPTH = 4
D_FF = 2816
EPS = 1e-6


class Prog:
    CE = ("pe", "act", "dve", "pool")
    NDMA = 48

    def __init__(self, nc):
        self.nc = nc
        self.ops = []
        self.kw = {}
        self.kr = {}
        self.NSW = 4
        self.dcount = [0] * (self.NDMA + self.NSW)
        self.ndma = 0
        self.nsw = 0

    def op(self, eng, fn, reads=(), writes=(), dma=False):
        deps = set()
        for k in reads:
            w = self.kw.get(k)
            if w is not None:
                deps.add(w)
            if isinstance(k, str) and k[:2] in ("ps", "pb"):
                r = self.kr.get(k)
                if r:
                    deps.update(v for e2, v in r[0].items() if e2 != eng)
        for k in writes:
            w = self.kw.get(k)
            if w is not None:
                deps.add(w)
            r = self.kr.get(k)
            if r:
                deps.update(r[0].values())
                deps.update(r[1])
        i = len(self.ops)
        o = dict(eng=eng, fn=fn, deps=deps, dma=dma, sig=False, cnt=0)
        if dma:
            if eng == "sp":
                s = self.ndma % self.NDMA
                self.ndma += 1
            else:
                s = self.NDMA + self.nsw % self.NSW
                self.nsw += 1
            self.dcount[s] += 1
            o["ds"] = s
            o["dv"] = 16 * self.dcount[s]
        self.ops.append(o)
        for k in reads:
            r = self.kr.setdefault(k, [{}, []])
            if dma:
                r[1].append(i)
            else:
                r[0][eng] = i
        for k in writes:
            self.kw[k] = i
            self.kr[k] = [{}, []]
        return i

    def dma(self, out, in_, r=(), w=(), eng="sp", **kw):
        return self.op(eng, lambda e: e.dma_start(out=out, in_=in_, **kw), r, w, dma=True)

    def barrier(self):
        last = {}
        dmas = []
        for i, o in enumerate(self.ops):
            if o.get("bar"):
                continue
            if o["dma"]:
                dmas.append(i)
            elif o["eng"] in self.CE:
                last[o["eng"]] = i
        deps = set(last.values()) | set(dmas[-2 * (self.NDMA + self.NSW):])
        for e in self.CE + ("sp",):
            self.ops.append(dict(eng=e, fn=lambda _e: None, deps=set(deps), dma=False, sig=False, cnt=0, bar=True))
        self.kw = {}
        self.kr = {}

    def finalize(self):
        ops = self.ops
        for o in ops:
            for d in o["deps"]:
                p = ops[d]
                if p["dma"]:
                    continue
                if p["eng"] == o["eng"] and not SAME_ENGINE_SYNC:
                    continue
                p["sig"] = True
        cnt = {e: 0 for e in self.CE}
        for o in ops:
            if o["sig"]:
                cnt[o["eng"]] += 1
                o["cnt"] = cnt[o["eng"]]
        waited = {e: {} for e in self.CE + ("sp",)}
        for o in ops:
            need = {}
            for d in o["deps"]:
                p = ops[d]
                if p["dma"]:
                    k = ("d", p["ds"])
                    v = p["dv"]
                else:
                    if p["eng"] == o["eng"] and not SAME_ENGINE_SYNC:
                        continue
                    k = ("c", p["eng"])
                    v = p["cnt"]
                if v > need.get(k, 0):
                    need[k] = v
            if o["dma"] and o["dv"] > 16:
                k = ("d", o["ds"])
                need[k] = max(need.get(k, 0), o["dv"] - 16)
            wl = []
            wd = waited[o["eng"]]
            for k, v in need.items():
                if wd.get(k, 0) < v:
                    wd[k] = v
                    wl.append((k, v))
            o["waits"] = wl
        self.cnt = cnt

    def emit(self):
        nc = self.nc
        self.finalize()
        with ExitStack() as st:
            sem = {("c", e): st.enter_context(nc.semaphore("s_" + e)) for e in self.CE}
            for i in range(self.NDMA + self.NSW):
                sem[("d", i)] = st.enter_context(nc.semaphore("d%d" % i))
            block = st.enter_context(nc.Block())
            per = {e: [o for o in self.ops if o["eng"] == e] for e in self.CE + ("sp",)}

            def run(e, lst):
                for o in lst:
                    for k, v in o["waits"]:
                        e.wait_ge(sem[k], v)
                    ins = o["fn"](e)
                    if ins is None:
                        continue
                    if o["dma"]:
                        ins.then_inc(sem[("d", o["ds"])], 16)
                    elif o["sig"]:
                        ins.then_inc(sem[("c", o["eng"])], 1)

            @block.tensor
            def _(e):
                run(e, per["pe"])

            @block.scalar
            def _(e):
                run(e, per["act"])

            @block.vector
            def _(e):
                run(e, per["dve"])

            @block.gpsimd
            def _(e):
                run(e, per["pool"])

            @block.sync
            def _(e):
                run(e, per["sp"])


def make_consts(S):
    bf = ml_dtypes.bfloat16
    c = {}
    idx = np.arange(128)
    a = idx[:, None]
    b = idx[None, :]
    c["c_identb"] = np.eye(128, dtype=np.float32).astype(bf)
    c["c_onesdiv"] = np.full((128, 128), 1.0 / D_MODEL, np.float32).astype(bf)
    c["c_ones32"] = np.ones((128, 128), np.float32)
    c["c_triu32"] = (a <= b).astype(np.float32)
    m2 = np.zeros((128, 2, 128), np.float32)
    m2[:, 0] = (a > b)
    m2[:, 1] = (a >= b)
    c["c_mask2"] = m2
    negm = np.zeros((128, 7, 2, 128), np.float32)
    for l in range(7):
        M = ((a >> (l + 1)) == (b >> (l + 1))) & (((a >> l) & 1) == 1) & (((b >> l) & 1) == 0)
        negm[:, l, 0] = -M.astype(np.float32)
        negm[:, l, 1] = -M.T.astype(np.float32)
    c["c_negm"] = negm.astype(bf)
    i2 = np.zeros((128, 2, 128), np.float32)
    i2[:, 0] = np.eye(128)
    i2[:, 1] = np.eye(128)
    c["c_ident2"] = i2.astype(bf)
    ma = np.zeros((128, 2, 128), np.float32)
    ma[:, 0] = (a >= b)
    ma[:, 1] = (a <= b)
    c["c_maska"] = ma.astype(bf)
    dd = idx % 64
    partner = np.where(dd < 8, idx + 8, np.where(dd < 16, idx - 8, idx))
    pm = np.zeros((128, 128), np.float32)
    pm[partner, idx] = 1.0
    c["c_pm"] = pm.astype(bf)
    sh = np.zeros((128, 2, 128), np.float32)
    for m in range(64):
        sh[m + 64, 0, m] = 1.0
        sh[m, 1, m + 64] = 1.0
    c["c_shift"] = sh
    pos = np.arange(S, dtype=np.float32)
    inv = (np.float32(500000.0) ** (-np.arange(0, 16, 2, dtype=np.float32) / np.float32(16))).astype(np.float32)
    ang = (pos[:, None] * inv[None, :]).astype(np.float32)
    cs, sn = np.cos(ang).astype(np.float32), np.sin(ang).astype(np.float32)
    C = np.ones((128, S), np.float32)
    Sg = np.zeros((128, S), np.float32)
    for p in range(128):
        d = p % 64
        if d < 8:
            C[p] = cs[:, d]
            Sg[p] = -sn[:, d]
        elif d < 16:
            C[p] = cs[:, d - 8]
            Sg[p] = sn[:, d - 8]
    c["ropeC"] = C
    c["ropeS"] = Sg
    return {k: np.ascontiguousarray(np.asarray(v).astype(np.float32)) for k, v in c.items()}


CONST_SHAPES = dict(c_identb=([128, 128], BF16), c_onesdiv=([128, 128], BF16), c_ones32=([128, 128], F32),
                    c_triu32=([128, 128], F32), c_mask2=([128, 2, 128], F32), c_negm=([128, 7, 2, 128], BF16),
                    c_ident2=([128, 2, 128], BF16), c_maska=([128, 2, 128], BF16), c_pm=([128, 128], BF16),
                    c_shift=([128, 2, 128], F32))


def layout_weights(inp, L):
    f = lambda x: np.ascontiguousarray(np.asarray(x, dtype=np.float32))
    w_in = f(inp["w_in"])[:L]
    o = {}
    cols = [i * 128 for i in range(16)] + [2056 + i * 128 for i in range(8)]
    wi = w_in.reshape(L, 8, 128, 3592)
    o["win_c"] = f(np.stack([wi[:, :, :, c0:c0 + 128] for c0 in cols], axis=1).transpose(0, 1, 3, 2, 4))
    o["wv"] = f(wi[:, :, :, 3080:3592].transpose(0, 2, 1, 3))
    o["wab"] = f(wi[:, :, :, 2048:2056].transpose(0, 2, 1, 3))
    o["wout"] = f(f(inp["w_out"])[:L].reshape(L, 8, 128, 1024).transpose(0, 2, 1, 3))
    up = f(inp["ffn_up"])[:L].reshape(L, 8, 128, 2, 22, 128)
    o["wup"] = f(up.transpose(0, 4, 2, 1, 3, 5))
    o["wdn"] = f(f(inp["ffn_down"])[:L].reshape(L, 22, 128, 1024).transpose(0, 2, 1, 3))
    nv = np.stack([f(inp[n])[:L] for n in ("pre_mix_norm", "post_mix_norm", "pre_ffn_norm", "post_ffn_norm")], axis=1)
    o["norms"] = f(nv.reshape(L, 4, 8, 128).transpose(3, 0, 1, 2))
    o["dnconv"] = f(f(inp["dn_conv"])[:L].reshape(L, 4, 12, 128).transpose(0, 3, 2, 1))
    o["fconv"] = f(f(inp["ffn_conv"])[:L].reshape(L, 3, 44, 128).transpose(0, 3, 2, 1))
    o["fbias"] = f(f(inp["ffn_conv_bias"])[:L].reshape(L, 44, 128).transpose(0, 2, 1))
    o["dnorm"] = f(f(inp["dn_out_norm"])[:L].reshape(L, 128, 1))
    o["alog"] = f(np.broadcast_to(f(inp["dn_a_log"])[:L, None, :], (L, 128, 4)))
    o["dtb"] = f(np.broadcast_to(f(inp["dn_dt_bias"])[:L, None, :], (L, 128, 4)))
    return o


def build_program(S, L, dbg=False, stop=None):
    NT = S // 128
    NTB = S // 512
    NB = S // 256
    nc = bass.Bass("TRN2", target_bir_lowering=False)
    P = Prog(nc)
    ES = ExitStack()

    def din(name, shape, dt=F32):
        return nc.dram_tensor(name, list(shape), dt, kind="ExternalInput").ap()

    def dscr(name, shape, dt, kind="Internal"):
        return nc.dram_tensor(name, list(shape), dt, kind=kind).ap()

    uid = [0]

    def sb(name, shape, dt, stack=None):
        uid[0] += 1
        return (stack or ES).enter_context(nc.sbuf_tensor("%s_%d" % (name, uid[0]), list(shape), dt))

    xT = din("xT", [128, 8, S])
    win_c = din("win_c", [L, 24, 128, 8, 128])
    wv_d = din("wv", [L, 128, 8, 512])
    wab_d = din("wab", [L, 128, 8, 8])
    wout_d = din("wout", [L, 128, 8, 1024])
    wup_d = din("wup", [L, 22, 128, 8, 2, 128])
    wdn_d = din("wdn", [L, 128, 22, 1024])
    norms_d = din("norms", [128, L, 4, 8])
    dnconv_d = din("dnconv", [L, 128, 12, 4])
    fconv_d = din("fconv", [L, 128, 44, 3])
    fbias_d = din("fbias", [L, 128, 44])
    dnorm_d = din("dnorm", [L, 128, 1])
    alog_d = din("alog", [L, 128, 4])
    dtb_d = din("dtb", [L, 128, 4])
    ropeC_d = din("ropeC", [128, S])
    ropeS_d = din("ropeS", [128, S])
    cd = {k: din(k, sh, F32) for k, (sh, dt) in CONST_SHAPES.items()}
    okind = "ExternalOutput"
    yT = dscr("yT", [128, 8, S], F32, kind=okind)
    dk = "ExternalOutput" if dbg else "Internal"
    qkvT = dscr("qkvT", [12, 128, S], BF16, kind=dk)
    zT = dscr("zT", [4, 128, S], BF16, kind=dk)
    qrT = dscr("qrT", [4, 128, S], BF16, kind=dk)
    krT = dscr("krT", [4, 128, S], BF16, kind=dk)
    vtok = dscr("vtok", [S, 512], BF16, kind=dk)
    aT = dscr("aT", [22, 128, S], BF16, kind=dk)
    if dbg:
        gdbg = dscr("gdbg", [128, 3, NT * 4], F32, kind=dk)
        mixdbg = dscr("mixdbg", [128, 8, S], BF16, kind=dk)

    cs = {k: sb("s_" + k, sh, dt) for k, (sh, dt) in CONST_SHAPES.items()}
    import os as _os
    NH_ = int(_os.environ.get("NH", "2"))
    _skip = set(_os.environ.get("SKIPC", "").split(","))
    for k in cs:
        if k in _skip:
            continue
        if CONST_SHAPES[k][1] != BF16:
            P.dma(cs[k][:], cd[k], w=[k])
    identb, onesdiv, ones32, triu32 = cs["c_identb"], cs["c_onesdiv"], cs["c_ones32"], cs["c_triu32"]
    mask2, negm, ident2, maska, pm, shiftm = cs["c_mask2"], cs["c_negm"], cs["c_ident2"], cs["c_maska"], cs["c_pm"], cs["c_shift"]
    CK = list(cs.keys())
    gam = sb("gam", [128, L, 4, 8], F32)
    P.dma(gam[:], norms_d, w=["gam"])
    dncv = sb("dncv", [128, 12, 4], F32)
    fcv = sb("fcv", [128, 44, 3], F32)
    fbs = sb("fbs", [128, 44], F32)
    dnr = sb("dnr", [128, 1], F32)
    alg = sb("alg", [128, 4], F32)
    dtbt = sb("dtbt", [128, 4], F32)
    negA = sb("negA", [128, 4], F32)
    gbraw = sb("gbraw", [128, NT, 8], F32)
    beta = sb("beta", [128, NT, 4], F32)
    gg = sb("gg", [128, NT, 4], F32)
    tmp4 = sb("tmp4", [128, NT, 4], F32)
    Gc = sb("Gc", [128, NT, 4], F32)
    Gl = sb("Gl", [128, NT, 4], F32)
    eG = sb("eG", [128, NT, 4], F32)
    eGL = sb("eGL", [128, NT, 4], F32)
    dcc = sb("dcc", [128, NT, 4], F32)
    Sst = sb("Sst", [128, 4, 128], F32)
    Sb_ = sb("Sb", [128, 4, 128], BF16)
    bufA = sb("bufA", [128, 8, S], BF16)
    X32 = H32 = sq = rs = None

    def alloc_norm(stack):
        nonlocal X32, H32, sq, rs
        X32 = sb("X32", [128, 8, 256], F32, stack)
        H32 = sb("H32", [128, 8, 256], F32, stack)
        sq = sb("sq", [128, 8, 256], BF16, stack)
        rs = sb("rs", [128, 256], F32, stack)
    ps = [ES.enter_context(nc.psum_tensor("ps%d" % i, [128, 512], F32)) for i in range(6)]
    pb = [ES.enter_context(nc.psum_tensor("pb%d" % i, [128, 1024], BF16)) for i in range(2)]
    psi = [0]

    def nps():
        i = psi[0] % 6
        psi[0] += 1
        return ps[i], "ps%d" % i

    vei = [0]

    def ve():
        vei[0] += 1
        return "dve" if vei[0] % 2 else "pool"

    def mmg(out, pairs, r, w):
        def fn(e):
            n = len(pairs)
            ins = None
            for i, (l, rr) in enumerate(pairs):
                ins = e.matmul(out, lhsT=l, rhs=rr, start=(i == 0), stop=(i == n - 1))
            return ins
        P.op("pe", fn, r, w)

    def mms(outs_pairs, r, w):
        def fn(e):
            ins = None
            for (o, l, rr) in outs_pairs:
                ins = e.matmul(o, lhsT=l, rhs=rr, start=True, stop=True)
            return ins
        P.op("pe", fn, r, w)

    def trs(outs_ins, r, w):
        def fn(e):
            ins = None
            for (o, i) in outs_ins:
                ins = e.transpose(out=o, in_=i, identity=identb[:])
            return ins
        P.op("pe", fn, list(r) + ["c_identb"], w)

    def ACT(out, in_, func, r, w, **kw):
        P.op("act", lambda e: e.activation(out=out, in_=in_, func=func, **kw), r, w)

    def TT(eng, out, in0, in1, op, r, w):
        P.op(eng, lambda e: e.tensor_tensor(out=out, in0=in0, in1=in1, op=op), r, w)

    def TS(eng, out, in0, s1, s2, op0, op1, r, w):
        if s2 is None:
            P.op(eng, lambda e: e.tensor_scalar(out=out, in0=in0, scalar1=s1, scalar2=None, op0=op0), r, w)
        else:
            P.op(eng, lambda e: e.tensor_scalar(out=out, in0=in0, scalar1=s1, scalar2=s2, op0=op0, op1=op1), r, w)

    def STT(eng, out, in0, scalar, in1, op0, op1, r, w):
        P.op("dve", lambda e: e.scalar_tensor_tensor(out=out, in0=in0, scalar=scalar, in1=in1, op0=op0, op1=op1), r, w)

    def CP(eng, out, in_, r, w):
        if eng == "act":
            P.op("act", lambda e: e.copy(out=out, in_=in_), r, w)
        else:
            P.op(eng, lambda e: e.tensor_copy(out=out, in_=in_), r, w)

    def MEMSET(eng, ap, val, w):
        P.op(eng, lambda e: e.memset(ap, val), (), w)

    def RED(out, in_, r, w):
        P.op("dve", lambda e: e.tensor_reduce(out=out, in_=in_, axis=AX.X, op=ALU.add), r, w)

    def RECIP(out, in_, r, w):
        P.op("dve", lambda e: e.reciprocal(out=out, in_=in_), r, w)

    wst = [sb("wst%d" % i, [128, 1024], F32) for i in range(2)]
    wsti = [0]

    def flat(ap):
        nd = len(ap.shape)
        if nd == 2:
            return ap
        if nd == 3:
            return ap.rearrange("p a b -> p (a b)")
        return ap.rearrange("p a b c -> p (a b c)")

    def load_cast(dst, src, w, r=()):
        d2, s2 = flat(dst), flat(src)
        n = d2.shape[1]
        for c in range(0, n, 1024):
            m = min(1024, n - c)
            j = wsti[0] % 2
            wsti[0] += 1
            P.dma(wst[j][:, 0:m], s2[:, c:c + m], w=["wst%d" % j])
            CP("pool", d2[:, c:c + m], wst[j][:, 0:m], ["wst%d" % j] + list(r), w)

    def bc(ap, shape):
        return ap.broadcast_to(list(shape))

    for k in cs:
        if k in _skip:
            continue
        if CONST_SHAPES[k][1] == BF16:
            load_cast(cs[k][:], cd[k], [k])

    def norm_to_h(gcol, hbuf, hkey, c0):
        ACT(sq[:], X32[:], AF.Square, ["X32"], ["sq"])
        pt, pk = nps()
        mmg(pt[:, 0:256], [(onesdiv[:], sq[:, k, :]) for k in range(8)], ["sq", "c_onesdiv"], [pk])
        ACT(rs[:], pt[:, 0:256], AF.Sqrt, [pk], ["rs"], bias=EPS, scale=1.0)
        RECIP(rs[:], rs[:], ["rs"], ["rs"])
        for k in range(8):
            STT(ve(), hbuf[:, k, c0:c0 + 256], X32[:, k, :], gcol[:, k:k + 1], rs[:], ALU.mult, ALU.mult,
                ["X32", "rs", "gam"], [(hkey, k, c0 // 256)])

    def post_resid(gpost, blk, hnext):
        c0 = blk * 256
        ACT(sq[:], H32[:], AF.Square, ["H32"], ["sq"])
        pt, pk = nps()
        mmg(pt[:, 0:256], [(onesdiv[:], sq[:, k, :]) for k in range(8)], ["sq", "c_onesdiv"], [pk])
        ACT(rs[:], pt[:, 0:256], AF.Sqrt, [pk], ["rs"], bias=EPS, scale=1.0)
        RECIP(rs[:], rs[:], ["rs"], ["rs"])
        TT("dve", H32[:], H32[:], bc(rs[:].unsqueeze(1), [128, 8, 256]), ALU.mult, ["H32", "rs"], ["H32"])
        for k in range(8):
            STT(ve(), X32[:, k, :], H32[:, k, :], gpost[:, k:k + 1], X32[:, k, :], ALU.mult, ALU.add,
                ["H32", "X32", "gam"], ["X32"])
        P.dma(yT[:, :, c0:c0 + 256], X32[:], r=["X32"], w=[("yT", blk)])
        if hnext is not None:
            norm_to_h(hnext[0], hnext[1], hnext[2], c0)

    with ExitStack() as W0:
        alloc_norm(W0)
        for blk in range(NB):
            P.dma(X32[:], xT[:, :, blk * 256:(blk + 1) * 256], w=["X32"])
            norm_to_h(gam[:, 0, 0, :], bufA, "A", blk * 256)
        P.barrier()
    AK = lambda k, c0, n: [("A", k, b) for b in range(c0 // 256, (c0 + n + 255) // 256)]
    BK = lambda k, c0, n: [("B", k, b) for b in range(c0 // 256, (c0 + n + 255) // 256)]

    class _Stop(Exception):
        pass

    def chk(name):
        if stop == name:
            raise _Stop()

    try:
        chk("P0")
        for l in range(L):
            P.barrier()
            P.dma(dncv[:], dnconv_d[l], w=["dncv"])
            P.dma(fcv[:], fconv_d[l], w=["fcv"])
            P.dma(fbs[:], fbias_d[l], w=["fbs"])
            P.dma(dnr[:], dnorm_d[l], w=["dnr"])
            P.dma(alg[:], alog_d[l], w=["alg"])
            P.dma(dtbt[:], dtb_d[l], w=["dtbt"])
            ACT(negA[:], alg[:], AF.Exp, ["alg"], ["negA"])
            TS("dve", negA[:], negA[:], -1.0, None, ALU.mult, None, ["negA"], ["negA"])
            xsrc = xT if l == 0 else yT

            with ExitStack() as W:
                wcb = [sb("wcb%d" % i, [128, 8, 128], BF16, W) for i in range(2)]
                NU = 4
                U = [sb("U%d" % i, [128, 515], F32, W) for i in range(NU)]
                Y = [sb("Y%d" % i, [128, 512], F32, W) for i in range(NU)]
                NYB = 6
                Yb = [sb("Yb%d" % i, [128, 512], BF16, W) for i in range(NYB)]
                ucnt = 0
                Cs = sb("Cs", [128, 512], F32, W)
                Ss = sb("Ss", [128, 512], F32, W)
                xb = sb("xb", [128, 512], BF16, W)
                t1 = sb("t1", [128, 512], F32, W)
                t2 = sb("t2", [128, 512], F32, W)
                wvb = sb("wvb", [128, 8, 512], BF16, W)
                wabb = sb("wabb", [128, 8, 8], BF16, W)
                ybi = 0
                for ci in range(24):
                    if ci == 1:
                        chk("M1a")
                    if ci == 2:
                        chk("M1b")
                    if ci == 12:
                        chk("M1q")
                    if ci == 16:
                        chk("M1y")
                    if ci == 17:
                        chk("M1s")
                    if ci == 18:
                        chk("M1s2")
                    if ci == 20:
                        chk("M1t")
                    if ci == 13:
                        chk("M1z")
                    wt = wcb[ci % 2]
                    wk = "wcb%d" % (ci % 2)
                    load_cast(wt[:], win_c[l, ci], [wk])
                    for tb in range(NTB):
                        c0 = tb * 512
                        pt, pk = nps()
                        mmg(pt[:], [(wt[:, k, :], bufA[:, k, c0:c0 + 512]) for k in range(8)],
                            [wk] + sum([AK(k, c0, 512) for k in range(8)], []), [pk])
                        yb = Yb[ybi % NYB]
                        ybk = "Yb%d" % (ybi % NYB)
                        ybi += 1
                        if ci < 12:
                            Uc, Un = U[ucnt % NU], U[(ucnt + 1) % NU]
                            uck, unk = "U%d" % (ucnt % NU), "U%d" % ((ucnt + 1) % NU)
                            if tb == 0:
                                MEMSET("pool", Uc[:, 0:3], 0.0, [uck + "h"])
                            CP("act", Uc[:, 3:515], pt[:], [pk], [uck])
                            if tb < NTB - 1:
                                CP("pool", Un[:, 0:3], Uc[:, 512:515], [uck], [unk + "h"])
                            yy = Y[ucnt % NU]
                            yk = "Y%d" % (ucnt % NU)
                            ucnt += 1
                            e1 = ve()
                            TS(e1, yy[:], Uc[:, 0:512], dncv[:, ci, 0:1], None, ALU.mult, None, [uck, uck + "h", "dncv"], [yk])
                            for j in range(1, 4):
                                STT(e1, yy[:], Uc[:, j:j + 512], dncv[:, ci, j:j + 1], yy[:], ALU.mult, ALU.add,
                                    [uck, uck + "h", "dncv", yk], [yk])
                            ACT(yb[:], yy[:], AF.Silu, [yk], [ybk])
                            P.dma(qkvT[ci, :, c0:c0 + 512], yb[:], r=[ybk], w=[("qkvT", ci, tb)])
                        elif ci < 16:
                            ACT(yb[:], pt[:], AF.Silu, [pk], [ybk])
                            P.dma(zT[ci - 12, :, c0:c0 + 512], yb[:], r=[ybk], w=[("zT", ci - 12, tb)])
                        else:
                            P.dma(Cs[:], ropeC_d[:, c0:c0 + 512], w=["Cs"])
                            P.dma(Ss[:], ropeS_d[:, c0:c0 + 512], w=["Ss"])
                            CP("act", xb[:], pt[:], [pk], ["xb"])
                            TT("dve", t1[:], pt[:], Cs[:], ALU.mult, [pk, "Cs", "xb"], ["t1"])
                            p2, p2k = nps()
                            mmg(p2[:], [(pm[:], xb[:])], ["xb", "c_pm"], [p2k])
                            TT("dve", t2[:], p2[:], Ss[:], ALU.mult, [p2k, "Ss"], ["t2"])
                            TT("pool", yb[:], t1[:], t2[:], ALU.add, ["t1", "t2"], [ybk])
                            if ci < 20:
                                P.dma(qrT[ci - 16, :, c0:c0 + 512], yb[:], r=[ybk], w=[("qrT", ci - 16, tb)])
                            else:
                                P.dma(krT[ci - 20, :, c0:c0 + 512], yb[:], r=[ybk], w=[("krT", ci - 20, tb)])
                chk("M1r")
                load_cast(wvb[:], wv_d[l], ["wvb"])
                for t in range(NT):
                    c0 = t * 128
                    pt, pk = nps()
                    mmg(pt[:], [(bufA[:, k, c0:c0 + 128], wvb[:, k, :]) for k in range(8)],
                        ["wvb"] + sum([AK(k, c0, 128) for k in range(8)], []), [pk])
                    yb = Yb[ybi % NYB]
                    ybk = "Yb%d" % (ybi % NYB)
                    ybi += 1
                    CP("act" if t % 2 else "dve", yb[:], pt[:], [pk], [ybk])
                    P.dma(vtok[c0:c0 + 128, :], yb[:], r=[ybk], w=[("vtok", t)])
                chk("M1v")
                load_cast(wabb[:], wab_d[l], ["wabb"])
                pt, pk = nps()
                for t in range(NT):
                    c0 = t * 128
                    mmg(pt[:, t * 8:(t + 1) * 8], [(bufA[:, k, c0:c0 + 128], wabb[:, k, :]) for k in range(8)],
                        ["wabb"] + sum([AK(k, c0, 128) for k in range(8)], []), [pk])
                CP("act", gbraw[:].rearrange("p t c -> p (t c)"), pt[:, 0:NT * 8], [pk], ["gbraw"])
                ACT(beta[:], gbraw[:, :, 0:4], AF.Sigmoid, ["gbraw"], ["beta"])
                TT("dve", tmp4[:], gbraw[:, :, 4:8], bc(dtbt[:].unsqueeze(1), [128, NT, 4]), ALU.add, ["gbraw", "dtbt"], ["tmp4"])
                ACT(tmp4[:], tmp4[:], AF.Exp, ["tmp4"], ["tmp4"])
                ACT(tmp4[:], tmp4[:], AF.Ln, ["tmp4"], ["tmp4"], bias=1.0, scale=1.0)
                TT("dve", gg[:], tmp4[:], bc(negA[:].unsqueeze(1), [128, NT, 4]), ALU.mult, ["tmp4", "negA"], ["gg"])
                pt, pk = nps()
                ggf = gg[:].rearrange("p t c -> p (t c)")
                mmg(pt[:, 0:NT * 4], [(triu32[:], ggf)], ["gg", "c_triu32"], [pk])
                CP("act", Gc[:].rearrange("p t c -> p (t c)"), pt[:, 0:NT * 4], [pk], ["Gc"])
                pt, pk = nps()
                mmg(pt[:, 0:NT * 4], [(ones32[:], ggf)], ["gg", "c_ones32"], [pk])
                CP("act", Gl[:].rearrange("p t c -> p (t c)"), pt[:, 0:NT * 4], [pk], ["Gl"])
                ACT(eG[:], Gc[:], AF.Exp, ["Gc"], ["eG"])
                ACT(dcc[:], Gl[:], AF.Exp, ["Gl"], ["dcc"])
                TT("dve", eGL[:], Gl[:], Gc[:], ALU.subtract, ["Gl", "Gc"], ["eGL"])
                ACT(eGL[:], eGL[:], AF.Exp, ["eGL"], ["eGL"])
                if dbg and l == 0:
                    P.dma(gdbg[:, 0, :], gg[:].rearrange("p t c -> p (t c)"), r=["gg"], w=["gdbg0"])
                    P.dma(gdbg[:, 1, :], beta[:].rearrange("p t c -> p (t c)"), r=["beta"], w=["gdbg1"])
                    P.dma(gdbg[:, 2, :], Gc[:].rearrange("p t c -> p (t c)"), r=["Gc"], w=["gdbg2"])
            P.barrier()
            chk("M1")

            with ExitStack() as W:
                qkvg = [sb("qkvg%d" % i, [128, 12, 512], BF16, W) for i in range(2)]
                zg = [sb("zg%d" % i, [128, 4, 512], BF16, W) for i in range(2)]
                sqt = sb("sqt", [128, 8, 128], F32, W)
                ssq = sb("ssq", [128, 8], F32, W)
                rn = sb("rn", [128, 8], F32, W)
                sck = sb("sck", [128, 4, 4], F32, W)
                scq = sb("scq", [128, 2, 4], F32, W)
                ktok4 = sb("ktok4", [128, 4, 4, 128], BF16, W)
                qtok2 = sb("qtok2", [128, 2, 4, 128], BF16, W)
                bv = sb("bv", [128, 4, 128], BF16, W)
                fT = sb("fT", [128, 4, 4, 128], BF16, W)
                TG = sb("TG", [128, 4, 128], F32, W)
                Dp = sb("Dp", [128, 4, 128], F32, W)
                gm = sb("gm", [128, 2, 4, 128], F32, W)
                AQ = sb("AQ", [128, 2, 4, 128], BF16, W)
                AQT = sb("AQT", [128, 2, 4, 128], BF16, W)
                D2 = sb("D2", [128, 2, 4, 128], BF16, W)
                tmpD = sb("tmpD", [128, 2, 4, 128], F32, W)
                Xb = sb("Xb", [128, 4, 128], BF16, W)
                WTb = sb("WTb", [128, 4, 128], BF16, W)
                U32 = sb("U32", [128, 4, 128], F32, W)
                vn = sb("vn", [128, 4, 128], BF16, W)
                osq = sb("osq", [128, 4, 128], F32, W)
                oss = sb("oss", [128, 4], F32, W)
                onb = sb("onb", [128, 4, 128], BF16, W)
                MEMSET("pool", Sst[:], 0.0, ["Sst"])
                MEMSET("pool", Sb_[:], 0.0, ["Sb"])
                for t in range(NT):
                    g4, tt = t // 4, t % 4
                    qg_, zg_ = qkvg[g4 % 2], zg[g4 % 2]
                    qgk, zgk = "qkvg%d" % (g4 % 2), "zg%d" % (g4 % 2)
                    if tt == 0:
                        c0 = g4 * 512
                        P.dma(qg_[:], qkvT[:, :, c0:c0 + 512].rearrange("c p n -> p c n"), w=[qgk])
                        P.dma(zg_[:], zT[:, :, c0:c0 + 512].rearrange("c p n -> p c n"), w=[zgk])
                    cc = slice(tt * 128, (tt + 1) * 128)
                    pbA = pb[0][:].rearrange("p (c n) -> p c n", n=128)
                    pbB = pb[1][:].rearrange("p (c n) -> p c n", n=128)
                    trs([(pbA[:, c, :], qg_[:, c, cc]) for c in range(8)], [qgk], ["pb0"])
                    trs([(pbB[:, c, :], qg_[:, 8 + c, cc]) for c in range(4)], [qgk], ["pb1"])
                    ACT(sqt[:], pbA, AF.Square, ["pb0"], ["sqt"])
                    RED(ssq[:], sqt[:], ["sqt"], ["ssq"])
                    ACT(rn[:], ssq[:], AF.Sqrt, ["ssq"], ["rn"], bias=EPS, scale=1.0)
                    RECIP(rn[:], rn[:], ["rn"], ["rn"])
                    bt, egt, eglt = beta[:, t, :], eG[:, t, :], eGL[:, t, :]
                    CP("pool", sck[:, 2, :], rn[:, 4:8], ["rn"], ["sck2"])
                    TT("dve", sck[:, 3, :], rn[:, 4:8], bt, ALU.mult, ["rn", "beta"], ["sck3"])
                    TT("dve", sck[:, 1, :], sck[:, 3, :], egt, ALU.mult, ["sck3", "eG"], ["sck1"])
                    TT("dve", sck[:, 0, :], rn[:, 4:8], eglt, ALU.mult, ["rn", "eGL"], ["sck0"])
                    TS("dve", scq[:, 0, :], rn[:, 0:4], float(128 ** -0.5), None, ALU.mult, None, ["rn"], ["scq0"])
                    TT("dve", scq[:, 1, :], scq[:, 0, :], egt, ALU.mult, ["scq0", "eG"], ["scq1"])
                    SK = ["sck0", "sck1", "sck2", "sck3"]
                    TT("dve", ktok4[:], bc(pbA[:, 4:8, :].unsqueeze(1), [128, 4, 4, 128]),
                       bc(sck[:].unsqueeze(3), [128, 4, 4, 128]), ALU.mult, ["pb0"] + SK, ["ktok4"])
                    TT("dve", qtok2[:], bc(pbA[:, 0:4, :].unsqueeze(1), [128, 2, 4, 128]),
                       bc(scq[:].unsqueeze(3), [128, 2, 4, 128]), ALU.mult, ["pb0", "scq0", "scq1"], ["qtok2"])
                    TT("dve", bv[:], pbB[:, 0:4, :], bc(bt.unsqueeze(2), [128, 4, 128]), ALU.mult, ["pb1", "beta"], ["bv"])
                    trs([(pbA[:, c, :], ktok4[:, 2, c, :]) for c in range(4)] +
                        [(pbA[:, 4 + c, :], ktok4[:, 3, c, :]) for c in range(4)], ["ktok4"], ["pb0"])
                    trs([(pbB[:, c, :], qtok2[:, 0, c, :]) for c in range(4)] +
                        [(pbB[:, 4 + c, :], qtok2[:, 1, c, :]) for c in range(4)], ["qtok2"], ["pb1"])
                    CP("act", fT[:, 0:2, :, :].rearrange("p a h n -> p (a h) n"), pbA, ["pb0"], ["fT01"])
                    CP("dve", fT[:, 2:4, :, :].rearrange("p a h n -> p (a h) n"), pbB, ["pb1"], ["fT23"])
                    pK, pKk = nps()
                    pQ, pQk = nps()
                    pK3 = pK[:].rearrange("p (h n) -> p h n", n=128)
                    pQ3 = pQ[:].rearrange("p (h n) -> p h n", n=128)
                    mms([(pK3[:, h, :], fT[:, 1, h, :], fT[:, 0, h, :]) for h in range(4)], ["fT01"], [pKk])
                    mms([(pQ3[:, h, :], fT[:, 2, h, :], fT[:, 0, h, :]) for h in range(4)], ["fT01", "fT23"], [pQk])
                    TT("pool", TG[:], bc(triu32[:].unsqueeze(1), [128, 4, 128]), bc(gg[:, t, :].unsqueeze(2), [128, 4, 128]),
                       ALU.mult, ["gg", "c_triu32"], ["TG"])
                    pG, pGk = nps()
                    pG3 = pG[:].rearrange("p (h n) -> p h n", n=128)
                    mms([(pG3[:, h, :], ones32[:], TG[:, h, :]) for h in range(4)], ["TG", "c_ones32"], [pGk])
                    TT("dve", Dp[:], pG3, bc(Gc[:, t, :].unsqueeze(2), [128, 4, 128]), ALU.subtract, [pGk, "Gc"], ["Dp"])
                    TT("pool", Dp[:], Dp[:], bc(mask2[:, 1, :].unsqueeze(1), [128, 4, 128]), ALU.mult, ["Dp", "c_mask2"], ["Dp"])
                    ACT(Dp[:], Dp[:], AF.Exp, ["Dp"], ["Dp"], scale=-1.0)
                    TT("pool", gm[:], bc(Dp[:].unsqueeze(1), [128, 2, 4, 128]), bc(mask2[:].unsqueeze(2), [128, 2, 4, 128]),
                       ALU.mult, ["Dp", "c_mask2"], ["gm"])
                    TT("dve", AQ[:, 0], pK3, gm[:, 0], ALU.mult, [pKk, "gm"], ["AQ0"])
                    TT("dve", AQ[:, 1], pQ3, gm[:, 1], ALU.mult, [pQk, "gm"], ["AQ1"])
                    trs([(pbA[:, c, :], AQ[:, 0, c, :]) for c in range(4)] +
                        [(pbA[:, 4 + c, :], AQ[:, 1, c, :]) for c in range(4)], ["AQ0", "AQ1"], ["pb0"])
                    CP("act", AQT[:].rearrange("p a h n -> p (a h) n"), pbA, ["pb0"], ["AQT"])
                    TT("pool", tmpD[:, 0], AQ[:, 0], bc(negm[:, 0, 0, :].unsqueeze(1), [128, 4, 128]), ALU.mult, ["AQ0", "c_negm"], ["tmpD0"])
                    TT("pool", tmpD[:, 1], AQT[:, 0], bc(negm[:, 0, 1, :].unsqueeze(1), [128, 4, 128]), ALU.mult, ["AQT", "c_negm"], ["tmpD1"])
                    TT("dve", D2[:], tmpD[:], bc(ident2[:].unsqueeze(2), [128, 2, 4, 128]), ALU.add, ["tmpD0", "tmpD1", "c_ident2"], ["D2"])
                    for lv in range(1, 7):
                        pX, pXk = nps()
                        pX3 = pX[:].rearrange("p (h n) -> p h n", n=128)
                        mms([(pX3[:, h, :], AQT[:, 0, h, :], D2[:, 0, h, :]) for h in range(4)], ["AQT", "D2"], [pXk])
                        CP("act", Xb[:], pX3, [pXk], ["Xb"])
                        pY, pYk = nps()
                        pZ, pZk = nps()
                        pY3 = pY[:].rearrange("p (h n) -> p h n", n=128)
                        pZ3 = pZ[:].rearrange("p (h n) -> p h n", n=128)
                        mms([(pY3[:, h, :], D2[:, 1, h, :], Xb[:, h, :]) for h in range(4)], ["D2", "Xb"], [pYk])
                        mms([(pZ3[:, h, :], Xb[:, h, :], D2[:, 1, h, :]) for h in range(4)], ["D2", "Xb"], [pZk])
                        TT("dve", tmpD[:, 0], pY3, bc(negm[:, lv, 0, :].unsqueeze(1), [128, 4, 128]), ALU.mult, [pYk, "c_negm"], ["tmpD0"])
                        TT("dve", tmpD[:, 1], pZ3, bc(negm[:, lv, 1, :].unsqueeze(1), [128, 4, 128]), ALU.mult, [pZk, "c_negm"], ["tmpD1"])
                        TT("pool", D2[:], D2[:], tmpD[:], ALU.add, ["tmpD0", "tmpD1", "D2"], ["D2"])
                    pW, pWk = nps()
                    pU, pUk = nps()
                    pW3 = pW[:].rearrange("p (h n) -> p h n", n=128)
                    pU3 = pU[:].rearrange("p (h n) -> p h n", n=128)
                    mms([(pW3[:, h, :], ktok4[:, 1, h, :], D2[:, 1, h, :]) for h in range(4)], ["ktok4", "D2"], [pWk])
                    mms([(pU3[:, h, :], D2[:, 1, h, :], bv[:, h, :]) for h in range(4)], ["bv", "D2"], [pUk])
                    CP("act", WTb[:], pW3, [pWk], ["WTb"])
                    CP("dve", U32[:], pU3, [pUk], ["U32"])
                    pS, pSk = nps()
                    pS3 = pS[:].rearrange("p (h n) -> p h n", n=128)
                    mms([(pS3[:, h, :], WTb[:, h, :], Sb_[:, h, :]) for h in range(4)], ["WTb", "Sb"], [pSk])
                    TT("dve", vn[:], U32[:], pS3, ALU.subtract, ["U32", pSk], ["vn"])
                    pO, pOk = nps()
                    pO3 = pO[:].rearrange("p (h n) -> p h n", n=128)

                    def ofn(e, pO3=pO3, fT=fT, AQT=AQT, vn=vn):
                        ins = None
                        for h in range(4):
                            e.matmul(pO3[:, h, :], lhsT=fT[:, 3, h, :], rhs=Sb_[:, h, :], start=True, stop=False)
                            ins = e.matmul(pO3[:, h, :], lhsT=AQT[:, 1, h, :], rhs=vn[:, h, :], start=False, stop=True)
                        return ins
                    P.op("pe", ofn, ["fT23", "Sb", "AQT", "vn"], [pOk])
                    pN, pNk = nps()
                    pN3 = pN[:].rearrange("p (h n) -> p h n", n=128)
                    mms([(pN3[:, h, :], ktok4[:, 0, h, :], vn[:, h, :]) for h in range(4)], ["ktok4", "vn"], [pNk])
                    TT("pool", Sst[:], Sst[:], bc(dcc[:, t, :].unsqueeze(2), [128, 4, 128]), ALU.mult, ["Sst", "dcc"], ["Sst"])
                    TT("dve", Sst[:], Sst[:], pN3, ALU.add, ["Sst", pNk], ["Sst"])
                    CP("act", Sb_[:], Sst[:], ["Sst"], ["Sb"])
                    ACT(osq[:], pO3, AF.Square, [pOk], ["osq"])
                    RED(oss[:], osq[:], ["osq"], ["oss"])
                    ACT(oss[:], oss[:], AF.Sqrt, ["oss"], ["oss"], bias=EPS, scale=1.0 / 128)
                    RECIP(oss[:], oss[:], ["oss"], ["oss"])
                    TT("dve", onb[:], pO3, bc(oss[:].unsqueeze(2), [128, 4, 128]), ALU.mult, [pOk, "oss"], ["onb"])
                    trs([(pbB[:, c, :], onb[:, c, :]) for c in range(4)], ["onb"], ["pb1"])
                    STT("dve", bufA[:, 0:4, t * 128:(t + 1) * 128], pbB[:, 0:4, :], dnr[:, 0:1], zg_[:, :, cc], ALU.mult, ALU.mult,
                        ["pb1", "dnr", zgk], [("A", k, t // 2) for k in range(4)])
            P.barrier()
            chk("M2")

            with ExitStack() as W:
                qh1 = sb("qh", [128, S], BF16, W)
                kz = [sb("kz%d" % i, [128, S], BF16, W) for i in range(2)]
                MEMSET("pool", kz[0][64:128, :], 0.0, ["kz0z"])
                MEMSET("pool", kz[1][0:64, :], 0.0, ["kz1z"])
                Oacc = sb("Oacc", [128, 2, S], F32, W)
                qdd = sb("qdd", [128, S], BF16, W)
                kdd = [sb("kdd%d" % i, [128, S], BF16, W) for i in range(2)]
                NV = 4
                Vt = [sb("Vt%d" % i, [128, 4, 64], BF16, W) for i in range(NV)]
                E = [sb("E%d" % i, [128, 2, 2, 128], BF16, W) for i in range(2)]
                Pt = [sb("Pt%d" % i, [128, 2, 2, 128], BF16, W) for i in range(2)]
                rden = sb("rden", [128, 512], F32, W)
                for i in range(NV):
                    MEMSET("pool", Vt[i][:, 1:3, :], 1.0, ["Vt%d" % i])
                vi = 0
                it = 0
                for hp in range(4):
                    qt, qk_ = qh1, "qh"
                    P.dma(qt[:], qrT[hp], w=[qk_])
                    P.dma(kz[0][0:64, :], krT[hp, 0:64, :], w=["kz0"])
                    P.dma(kz[1][64:128, :], krT[hp, 64:128, :], w=["kz1"])
                    for ci_, dil in enumerate((1, 4, 16)):
                        Lr = S // dil
                        nb = Lr // 128
                        if dil == 1:
                            qsrc, ksrc, qsk, ksk = qt, kz, qk_, ["kz0", "kz1", "kz0z", "kz1z"]
                        else:
                            CP("pool", qdd[:].rearrange("p (r i) -> p r i", r=dil), qt[:].rearrange("p (i r) -> p r i", r=dil), [qk_], ["qdd"])
                            CP("act", kdd[0][:].rearrange("p (r i) -> p r i", r=dil), kz[0][:].rearrange("p (i r) -> p r i", r=dil), ["kz0", "kz0z"], ["kdd0"])
                            CP("pool", kdd[1][:].rearrange("p (r i) -> p r i", r=dil), kz[1][:].rearrange("p (i r) -> p r i", r=dil), ["kz1", "kz1z"], ["kdd1"])
                            qsrc, ksrc, qsk, ksk = qdd, kdd, "qdd", ["kdd0", "kdd1"]
                        for r in range(dil):
                            prev = None
                            for b in range(nb):
                                st_ = 128 * b * dil + r
                                sl = slice(st_, st_ + 127 * dil + 1, dil)
                                vt = Vt[vi % NV]
                                vk = "Vt%d" % (vi % NV)
                                vi += 1
                                P.dma(vt[:, 0:4:3, :], vtok[sl, hp * 128:(hp + 1) * 128].rearrange("n (h d) -> n h d", h=2),
                                      w=[vk + "v"], r=[vk])
                                pt, pk = nps()
                                p4 = pt[:].rearrange("p (h k n) -> p h k n", h=2, k=2)
                                cl = slice(r * Lr + 128 * b, r * Lr + 128 * b + 128)
                                kbs = ([(0, prev)] if prev is not None else []) + [(1, (cl, vt, vk))]
                                mms([(p4[:, h, kb, :], ksrc[h][:, blk[0]], qsrc[:, cl])
                                     for h in range(NH_) for (kb, blk) in kbs], [qsk] + ksk, [pk])
                                e_, ek = E[it % 2], "E%d" % (it % 2)
                                p_, pk2 = Pt[it % 2], "Pt%d" % (it % 2)
                                it += 1
                                k0 = 0 if prev is not None else 1
                                ACT(e_[:, :, k0:2, :], p4[:, :, k0:2, :], AF.Exp, [pk], [ek], scale=0.125)
                                TT("pool", p_[:, :, k0:2, :], e_[:, :, k0:2, :], bc(maska[:, k0:2, :].unsqueeze(1), [128, 2, 2 - k0, 128]),
                                   ALU.mult, [ek, "c_maska"], [pk2])
                                po, pok = nps()
                                po3 = po[:, 0:256].rearrange("p (h n) -> p h n", h=2)

                                def pvfn(e, po3=po3, kbs=kbs, p_=p_):
                                    ins = None
                                    for h in range(NH_):
                                        for i, (kb, blk) in enumerate(kbs):
                                            ins = e.matmul(po3[:, h, :], lhsT=blk[1][:, 2 * h:2 * h + 2, :].rearrange("p a d -> p (a d)"),
                                                           rhs=p_[:, h, kb, :], start=(i == 0), stop=(i == len(kbs) - 1))
                                    return ins
                                P.op("pe", pvfn, [pk2] + [blk[2] + "v" for (_, blk) in kbs] + [blk[2] for (_, blk) in kbs], [pok])
                                if ci_ == 0:
                                    CP("dve", Oacc[:, :, sl], po3, [pok], [("Oacc", b // 2)])
                                else:
                                    okeys = [("Oacc", x) for x in range(st_ // 256, (st_ + 128 * dil + 255) // 256)]
                                    TT("dve", Oacc[:, :, sl], Oacc[:, :, sl], po3, ALU.add, [pok] + okeys, okeys)
                                prev = (cl, vt, vk)
                                if hp == 0 and ci_ == 0 and b == 0:
                                    chk("M3a")
                                if hp == 0 and ci_ == 0 and b == 1:
                                    chk("M3b")
                        if hp == 0:
                            chk("M3c%d" % ci_)
                    for tb in range(NTB):
                        c0 = tb * 512
                        for h in range(2):
                            pd, pdk = nps()
                            mmg(pd[:], [(shiftm[:, h, :], Oacc[:, h, c0:c0 + 512])],
                                ["c_shift", ("Oacc", 2 * tb), ("Oacc", 2 * tb + 1)], [pdk])
                            hs = slice(h * 64, (h + 1) * 64)
                            RECIP(rden[hs, :], pd[hs, :], [pdk], ["rden%d" % h])
                            TT("pool", bufA[hs, 4 + hp, c0:c0 + 512], Oacc[hs, h, c0:c0 + 512], rden[hs, :], ALU.mult,
                               ["rden%d" % h, ("Oacc", 2 * tb), ("Oacc", 2 * tb + 1)], [("A", 4 + hp, 2 * tb, h), ("A", 4 + hp, 2 * tb + 1, h)])
                    if hp == 0:
                        chk("M3f")
                if dbg and l == 0:
                    P.barrier()
                    P.dma(mixdbg, bufA[:], w=["mixdbg"])
            P.barrier()
            chk("M3")

            with ExitStack() as WB:
                bufB = sb("bufB", [128, 8, S], BF16, WB)
                with ExitStack() as W:
                    woutb = sb("woutb", [128, 8, 1024], BF16, W)
                    alloc_norm(W)
                    for k in range(8):
                        load_cast(woutb[:, k, :], wout_d[l, :, k, :], ["woutb%d" % k])
                    for blk in range(NB):
                        c0 = blk * 256
                        for c in range(8):
                            pt, pk = nps()
                            mmg(pt[:, 0:256], [(woutb[:, k, c * 128:(c + 1) * 128], bufA[:, k, c0:c0 + 256]) for k in range(8)],
                                ["woutb%d" % k for k in range(8)], [pk])
                            CP("act" if c % 2 else "dve", H32[:, c, :], pt[:, 0:256], [pk], ["H32"])
                        P.dma(X32[:], xsrc[:, :, c0:c0 + 256], w=["X32"])
                        post_resid(gam[:, l, 1, :], blk, (gam[:, l, 2, :], bufB, "B"))
                P.barrier()
                with ExitStack() as W:
                    wjb = [sb("wjb%d" % i, [128, 8, 2, 128], BF16, W) for i in range(2)]
                    NF = 3
                    Ug = [sb("Ug%d" % i, [128, 514], F32, W) for i in range(NF)]
                    Uv = [sb("Uv%d" % i, [128, 514], F32, W) for i in range(NF)]
                    Yg = [sb("Yg%d" % i, [128, 512], F32, W) for i in range(NF)]
                    Yv = [sb("Yv%d" % i, [128, 512], F32, W) for i in range(NF)]
                    g1s = [sb("g1_%d" % i, [128, 512], F32, W) for i in range(NF)]
                    g2s = [sb("g2_%d" % i, [128, 512], F32, W) for i in range(NF)]
                    ab = [sb("ab%d" % i, [128, 512], BF16, W) for i in range(NF)]
                    it = 0
                    for j in range(22):
                        wt, wk = wjb[j % 2], "wjb%d" % (j % 2)
                        load_cast(wt[:], wup_d[l, j], [wk])
                        for tb in range(NTB):
                            c0 = tb * 512
                            i2 = it % NF
                            i3 = (it + 1) % NF
                            it += 1
                            g1, g2 = g1s[i2], g2s[i2]
                            g1k, g2k = "g1_%d" % i2, "g2_%d" % i2
                            pg, pgk = nps()
                            pv, pvk = nps()
                            bkeys = sum([BK(k, c0, 512) for k in range(8)], [])
                            mmg(pg[:], [(wt[:, k, 0, :], bufB[:, k, c0:c0 + 512]) for k in range(8)], [wk] + bkeys, [pgk])
                            mmg(pv[:], [(wt[:, k, 1, :], bufB[:, k, c0:c0 + 512]) for k in range(8)], [wk] + bkeys, [pvk])
                            for (Ux, px, pxk, nm, cj, Yx) in ((Ug, pg, pgk, "Ug", j, Yg), (Uv, pv, pvk, "Uv", 22 + j, Yv)):
                                Uc, Un = Ux[i2], Ux[i3]
                                uck, unk = nm + str(i2), nm + str(i3)
                                if tb == 0:
                                    MEMSET("pool", Uc[:, 0:2], 0.0, [uck + "h"])
                                CP("act" if nm == "Ug" else "dve", Uc[:, 2:514], px[:], [pxk], [uck])
                                if tb < NTB - 1:
                                    CP("pool", Un[:, 0:2], Uc[:, 512:514], [uck], [unk + "h"])
                                yy, yk = Yx[i2], nm + "Y%d" % i2
                                en = "pool" if nm == "Ug" else "dve"
                                TS(en, yy[:], Uc[:, 0:512], fcv[:, cj, 0:1], fbs[:, cj:cj + 1], ALU.mult, ALU.add,
                                   [uck, uck + "h", "fcv", "fbs"], [yk])
                                for jj in range(1, 3):
                                    STT(en, yy[:], Uc[:, jj:jj + 512], fcv[:, cj, jj:jj + 1], yy[:], ALU.mult, ALU.add,
                                        [uck, uck + "h", "fcv", yk], [yk])
                            yg, yv = Yg[i2], Yv[i2]
                            ygk, yvk = "UgY%d" % i2, "UvY%d" % i2
                            ACT(g1[:], yg[:], AF.Square, [ygk], [g1k])
                            TS("dve", g1[:], g1[:], 0.044715, 1.0, ALU.mult, ALU.add, [g1k], [g1k])
                            TT("pool", g1[:], g1[:], yg[:], ALU.mult, [g1k, ygk], [g1k])
                            ACT(g1[:], g1[:], AF.Sigmoid, [g1k], [g1k], scale=1.5957691216057308)
                            TT("pool", g2[:], yg[:], yv[:], ALU.mult, [ygk, yvk], [g2k])
                            TT("dve", ab[i2][:], g2[:], g1[:], ALU.mult, [g1k, g2k], ["ab%d" % i2])
                            P.dma(aT[j, :, c0:c0 + 512], ab[i2][:], r=["ab%d" % i2], w=[("aT", j, tb)])
            P.barrier()
            chk("F1")

            with ExitStack() as W:
                wdb = sb("wdb", [128, 22, 1024], BF16, W)
                alloc_norm(W)
                ablk = [sb("ablk%d" % i, [128, 22, 256], BF16, W) for i in range(2)]
                for k in range(22):
                    load_cast(wdb[:, k, :], wdn_d[l, :, k, :], ["wdb%d" % k])
                for blk in range(NB):
                    c0 = blk * 256
                    at, atk = ablk[blk % 2], "ablk%d" % (blk % 2)
                    P.dma(at[:], aT[:, :, c0:c0 + 256].rearrange("c p n -> p c n"), w=[atk])
                    for c in range(8):
                        pt, pk = nps()
                        mmg(pt[:, 0:256], [(wdb[:, k, c * 128:(c + 1) * 128], at[:, k, :]) for k in range(22)],
                            [atk] + ["wdb%d" % k for k in range(22)], [pk])
                        CP("act" if c % 2 else "dve", H32[:, c, :], pt[:, 0:256], [pk], ["H32"])
                    P.dma(X32[:], yT[:, :, c0:c0 + 256], w=["X32"], r=[("yT", blk)])
                    nxt = (gam[:, l + 1, 0, :], bufA, "A") if l + 1 < L else None
                    post_resid(gam[:, l, 3, :], blk, nxt)

    except _Stop:
        pass
    P.barrier()
    P.emit()
    try:
        ES.close()
    except AssertionError:
        pass
    return nc, P


_CACHE = {}


def kernel(**inputs):
    x = np.asarray(inputs["x"], dtype=np.float32)
    B, S, D = x.shape
    L = DEPTH
    key = (S, L)
    if key not in _CACHE:
        _CACHE[key] = build_program(S, L)
    nc, _ = _CACHE[key]
    shared = layout_weights(inputs, L)
    shared.update(make_consts(S))
    in_maps = []
    for b in range(B):
        m = dict(shared)
        m["xT"] = np.ascontiguousarray(x[b].T.reshape(8, 128, S).transpose(1, 0, 2))
        in_maps.append(m)
    out = np.empty((B, S, D), np.float32)
    if MULTI_CORE:
        res = run_bass_kernel_spmd(nc, in_maps, core_ids=list(range(B)))
        results = res.results
    else:
        results = [run_bass_kernel_spmd(nc, [in_maps[b]], core_ids=[0]).results[0] for b in range(B)]
    for b in range(B):
        yT = np.asarray(results[b]["yT"], dtype=np.float32)
        out[b] = yT.transpose(1, 0, 2).reshape(D, S).T
    return out
```

```python
import numpy as np
import ml_dtypes
from contextlib import ExitStack
import concourse.bass as bass
import concourse.mybir as mybir
from concourse.bass_utils import run_bass_kernel_spmd

F32 = mybir.dt.float32
BF16 = mybir.dt.bfloat16
AF = mybir.ActivationFunctionType
ALU = mybir.AluOpType
AX = mybir.AxisListType

SAME_ENGINE_SYNC = True
MULTI_CORE = True
D_MODEL = 1024
DEPTH = 4
D_FF = 2816
EPS = 1e-6


class Prog:
    CE = ("pe", "act", "dve", "pool")
    NDMA = 48

    def __init__(self, nc):
        self.nc = nc
        self.ops = []
        self.kw = {}
        self.kr = {}
        self.NSW = 4
        self.dcount = [0] * (self.NDMA + self.NSW)
        self.ndma = 0
        self.nsw = 0

    def op(self, eng, fn, reads=(), writes=(), dma=False):
        deps = set()
        for k in reads:
            w = self.kw.get(k)
            if w is not None:
                deps.add(w)
            if isinstance(k, str) and k[:2] in ("ps", "pb"):
                r = self.kr.get(k)
                if r:
                    deps.update(v for e2, v in r[0].items() if e2 != eng)
        for k in writes:
            w = self.kw.get(k)
            if w is not None:
                deps.add(w)
            r = self.kr.get(k)
            if r:
                deps.update(r[0].values())
                deps.update(r[1])
        i = len(self.ops)
        o = dict(eng=eng, fn=fn, deps=deps, dma=dma, sig=False, cnt=0)
        if dma:
            if eng == "sp":
                s = self.ndma % self.NDMA
                self.ndma += 1
            else:
                s = self.NDMA + self.nsw % self.NSW
                self.nsw += 1
            self.dcount[s] += 1
            o["ds"] = s
            o["dv"] = 16 * self.dcount[s]
        self.ops.append(o)
        for k in reads:
            r = self.kr.setdefault(k, [{}, []])
            if dma:
                r[1].append(i)
            else:
                r[0][eng] = i
        for k in writes:
            self.kw[k] = i
            self.kr[k] = [{}, []]
        return i

    def dma(self, out, in_, r=(), w=(), eng="sp", **kw):
        return self.op(eng, lambda e: e.dma_start(out=out, in_=in_, **kw), r, w, dma=True)

    def barrier(self):
        last = {}
        dmas = []
        for i, o in enumerate(self.ops):
            if o.get("bar"):
                continue
            if o["dma"]:
                dmas.append(i)
            elif o["eng"] in self.CE:
                last[o["eng"]] = i
        deps = set(last.values()) | set(dmas[-2 * (self.NDMA + self.NSW):])
        for e in self.CE + ("sp",):
            self.ops.append(dict(eng=e, fn=lambda _e: None, deps=set(deps), dma=False, sig=False, cnt=0, bar=True))
        self.kw = {}
        self.kr = {}

    def finalize(self):
        ops = self.ops
        for o in ops:
            for d in o["deps"]:
                p = ops[d]
                if p["dma"]:
                    continue
                if p["eng"] == o["eng"] and not SAME_ENGINE_SYNC:
                    continue
                p["sig"] = True
        cnt = {e: 0 for e in self.CE}
        for o in ops:
            if o["sig"]:
                cnt[o["eng"]] += 1
                o["cnt"] = cnt[o["eng"]]
        waited = {e: {} for e in self.CE + ("sp",)}
        for o in ops:
            need = {}
            for d in o["deps"]:
                p = ops[d]
                if p["dma"]:
                    k = ("d", p["ds"])
                    v = p["dv"]
                else:
                    if p["eng"] == o["eng"] and not SAME_ENGINE_SYNC:
                        continue
                    k = ("c", p["eng"])
                    v = p["cnt"]
                if v > need.get(k, 0):
                    need[k] = v
            if o["dma"] and o["dv"] > 16:
                k = ("d", o["ds"])
                need[k] = max(need.get(k, 0), o["dv"] - 16)
            wl = []
            wd = waited[o["eng"]]
            for k, v in need.items():
                if wd.get(k, 0) < v:
                    wd[k] = v
                    wl.append((k, v))
            o["waits"] = wl
        self.cnt = cnt

    def emit(self):
        nc = self.nc
        self.finalize()
        with ExitStack() as st:
            sem = {("c", e): st.enter_context(nc.semaphore("s_" + e)) for e in self.CE}
            for i in range(self.NDMA + self.NSW):
                sem[("d", i)] = st.enter_context(nc.semaphore("d%d" % i))
            block = st.enter_context(nc.Block())
            per = {e: [o for o in self.ops if o["eng"] == e] for e in self.CE + ("sp",)}

            def run(e, lst):
                for o in lst:
                    for k, v in o["waits"]:
                        e.wait_ge(sem[k], v)
                    ins = o["fn"](e)
                    if ins is None:
                        continue
                    if o["dma"]:
                        ins.then_inc(sem[("d", o["ds"])], 16)
                    elif o["sig"]:
                        ins.then_inc(sem[("c", o["eng"])], 1)

            @block.tensor
            def _(e):
                run(e, per["pe"])

            @block.scalar
            def _(e):
                run(e, per["act"])

            @block.vector
            def _(e):
                run(e, per["dve"])

            @block.gpsimd
            def _(e):
                run(e, per["pool"])

            @block.sync
            def _(e):
                run(e, per["sp"])


def make_consts(S):
    bf = ml_dtypes.bfloat16
    c = {}
    idx = np.arange(128)
    a = idx[:, None]
    b = idx[None, :]
    c["c_identb"] = np.eye(128, dtype=np.float32).astype(bf)
    c["c_onesdiv"] = np.full((128, 128), 1.0 / D_MODEL, np.float32).astype(bf)
    c["c_ones32"] = np.ones((128, 128), np.float32)
    c["c_triu32"] = (a <= b).astype(np.float32)
    m2 = np.zeros((128, 2, 128), np.float32)
    m2[:, 0] = (a > b)
    m2[:, 1] = (a >= b)
    c["c_mask2"] = m2
    negm = np.zeros((128, 7, 2, 128), np.float32)
    for l in range(7):
        M = ((a >> (l + 1)) == (b >> (l + 1))) & (((a >> l) & 1) == 1) & (((b >> l) & 1) == 0)
        negm[:, l, 0] = -M.astype(np.float32)
        negm[:, l, 1] = -M.T.astype(np.float32)
    c["c_negm"] = negm.astype(bf)
    i2 = np.zeros((128, 2, 128), np.float32)
    i2[:, 0] = np.eye(128)
    i2[:, 1] = np.eye(128)
    c["c_ident2"] = i2.astype(bf)
    ma = np.zeros((128, 2, 128), np.float32)
    ma[:, 0] = (a >= b)
    ma[:, 1] = (a <= b)
    c["c_maska"] = ma.astype(bf)
    dd = idx % 64
    partner = np.where(dd < 8, idx + 8, np.where(dd < 16, idx - 8, idx))
    pm = np.zeros((128, 128), np.float32)
    pm[partner, idx] = 1.0
    c["c_pm"] = pm.astype(bf)
    sh = np.zeros((128, 2, 128), np.float32)
    for m in range(64):
        sh[m + 64, 0, m] = 1.0
        sh[m, 1, m + 64] = 1.0
    c["c_shift"] = sh
    pos = np.arange(S, dtype=np.float32)
    inv = (np.float32(500000.0) ** (-np.arange(0, 16, 2, dtype=np.float32) / np.float32(16))).astype(np.float32)
    ang = (pos[:, None] * inv[None, :]).astype(np.float32)
    cs, sn = np.cos(ang).astype(np.float32), np.sin(ang).astype(np.float32)
    C = np.ones((128, S), np.float32)
    Sg = np.zeros((128, S), np.float32)
    for p in range(128):
        d = p % 64
        if d < 8:
            C[p] = cs[:, d]
            Sg[p] = -sn[:, d]
        elif d < 16:
            C[p] = cs[:, d - 8]
            Sg[p] = sn[:, d - 8]
    c["ropeC"] = C
    c["ropeS"] = Sg
    return {k: np.ascontiguousarray(np.asarray(v).astype(np.float32)) for k, v in c.items()}


CONST_SHAPES = dict(c_identb=([128, 128], BF16), c_onesdiv=([128, 128], BF16), c_ones32=([128, 128], F32),
                    c_triu32=([128, 128], F32), c_mask2=([128, 2, 128], F32), c_negm=([128, 7, 2, 128], BF16),
                    c_ident2=([128, 2, 128], BF16), c_maska=([128, 2, 128], BF16), c_pm=([128, 128], BF16),
                    c_shift=([128, 2, 128], F32))


def layout_weights(inp, L):
    f = lambda x: np.ascontiguousarray(np.asarray(x, dtype=np.float32))
    w_in = f(inp["w_in"])[:L]
    o = {}
    cols = [i * 128 for i in range(16)] + [2056 + i * 128 for i in range(8)]
    wi = w_in.reshape(L, 8, 128, 3592)
    o["win_c"] = f(np.stack([wi[:, :, :, c0:c0 + 128] for c0 in cols], axis=1).transpose(0, 1, 3, 2, 4))
    o["wv"] = f(wi[:, :, :, 3080:3592].transpose(0, 2, 1, 3))
    o["wab"] = f(wi[:, :, :, 2048:2056].transpose(0, 2, 1, 3))
    o["wout"] = f(f(inp["w_out"])[:L].reshape(L, 8, 128, 1024).transpose(0, 2, 1, 3))
    up = f(inp["ffn_up"])[:L].reshape(L, 8, 128, 2, 22, 128)
    o["wup"] = f(up.transpose(0, 4, 2, 1, 3, 5))
    o["wdn"] = f(f(inp["ffn_down"])[:L].reshape(L, 22, 128, 1024).transpose(0, 2, 1, 3))
    nv = np.stack([f(inp[n])[:L] for n in ("pre_mix_norm", "post_mix_norm", "pre_ffn_norm", "post_ffn_norm")], axis=1)
    o["norms"] = f(nv.reshape(L, 4, 8, 128).transpose(3, 0, 1, 2))
    o["dnconv"] = f(f(inp["dn_conv"])[:L].reshape(L, 4, 12, 128).transpose(0, 3, 2, 1))
    o["fconv"] = f(f(inp["ffn_conv"])[:L].reshape(L, 3, 44, 128).transpose(0, 3, 2, 1))
    o["fbias"] = f(f(inp["ffn_conv_bias"])[:L].reshape(L, 44, 128).transpose(0, 2, 1))
    o["dnorm"] = f(f(inp["dn_out_norm"])[:L].reshape(L, 128, 1))
    o["alog"] = f(np.broadcast_to(f(inp["dn_a_log"])[:L, None, :], (L, 128, 4)))
    o["dtb"] = f(np.broadcast_to(f(inp["dn_dt_bias"])[:L, None, :], (L, 128, 4)))
    return o


def build_program(S, L, dbg=False, stop=None):
    NT = S // 128
    NTB = S // 512
    NB = S // 256
    nc = bass.Bass("TRN2", target_bir_lowering=False)
    P = Prog(nc)
    ES = ExitStack()

    def din(name, shape, dt=F32):
        return nc.dram_tensor(name, list(shape), dt, kind="ExternalInput").ap()

    def dscr(name, shape, dt, kind="Internal"):
        return nc.dram_tensor(name, list(shape), dt, kind=kind).ap()

    uid = [0]

    def sb(name, shape, dt, stack=None):
        uid[0] += 1
        return (stack or ES).enter_context(nc.sbuf_tensor("%s_%d" % (name, uid[0]), list(shape), dt))

    xT = din("xT", [128, 8, S])
    win_c = din("win_c", [L, 24, 128, 8, 128])
    wv_d = din("wv", [L, 128, 8, 512])
    wab_d = din("wab", [L, 128, 8, 8])
    wout_d = din("wout", [L, 128, 8, 1024])
    wup_d = din("wup", [L, 22, 128, 8, 2, 128])
    wdn_d = din("wdn", [L, 128, 22, 1024])
    norms_d = din("norms", [128, L, 4, 8])
    dnconv_d = din("dnconv", [L, 128, 12, 4])
    fconv_d = din("fconv", [L, 128, 44, 3])
    fbias_d = din("fbias", [L, 128, 44])
    dnorm_d = din("dnorm", [L, 128, 1])
    alog_d = din("alog", [L, 128, 4])
    dtb_d = din("dtb", [L, 128, 4])
    ropeC_d = din("ropeC", [128, S])
    ropeS_d = din("ropeS", [128, S])
    cd = {k: din(k, sh, F32) for k, (sh, dt) in CONST_SHAPES.items()}
    okind = "ExternalOutput"
    yT = dscr("yT", [128, 8, S], F32, kind=okind)
    dk = "ExternalOutput" if dbg else "Internal"
    qkvT = dscr("qkvT", [12, 128, S], BF16, kind=dk)
    zT = dscr("zT", [4, 128, S], BF16, kind=dk)
    qrT = dscr("qrT", [4, 128, S], BF16, kind=dk)
    krT = dscr("krT", [4, 128, S], BF16, kind=dk)
    vtok = dscr("vtok", [S, 512], BF16, kind=dk)
    aT = dscr("aT", [22, 128, S], BF16, kind=dk)
    if dbg:
        gdbg = dscr("gdbg", [128, 3, NT * 4], F32, kind=dk)
        mixdbg = dscr("mixdbg", [128, 8, S], BF16, kind=dk)

    cs = {k: sb("s_" + k, sh, dt) for k, (sh, dt) in CONST_SHAPES.items()}
    import os as _os
    NH_ = int(_os.environ.get("NH", "2"))
    _skip = set(_os.environ.get("SKIPC", "").split(","))
    for k in cs:
        if k in _skip:
            continue
        if CONST_SHAPES[k][1] != BF16:
            P.dma(cs[k][:], cd[k], w=[k])
    identb, onesdiv, ones32, triu32 = cs["c_identb"], cs["c_onesdiv"], cs["c_ones32"], cs["c_triu32"]
    mask2, negm, ident2, maska, pm, shiftm = cs["c_mask2"], cs["c_negm"], cs["c_ident2"], cs["c_maska"], cs["c_pm"], cs["c_shift"]
    CK = list(cs.keys())
    gam = sb("gam", [128, L, 4, 8], F32)
    P.dma(gam[:], norms_d, w=["gam"])
    dncv = sb("dncv", [128, 12, 4], F32)
    fcv = sb("fcv", [128, 44, 3], F32)
    fbs = sb("fbs", [128, 44], F32)
    dnr = sb("dnr", [128, 1], F32)
    alg = sb("alg", [128, 4], F32)
    dtbt = sb("dtbt", [128, 4], F32)
    negA = sb("negA", [128, 4], F32)
    gbraw = sb("gbraw", [128, NT, 8], F32)
    beta = sb("beta", [128, NT, 4], F32)
    gg = sb("gg", [128, NT, 4], F32)
    tmp4 = sb("tmp4", [128, NT, 4], F32)
    Gc = sb("Gc", [128, NT, 4], F32)
    Gl = sb("Gl", [128, NT, 4], F32)
    eG = sb("eG", [128, NT, 4], F32)
    eGL = sb("eGL", [128, NT, 4], F32)
    dcc = sb("dcc", [128, NT, 4], F32)
    Sst = sb("Sst", [128, 4, 128], F32)
    Sb_ = sb("Sb", [128, 4, 128], BF16)
    bufA = sb("bufA", [128, 8, S], BF16)
    X32 = H32 = sq = rs = None

    def alloc_norm(stack):
        nonlocal X32, H32, sq, rs
        X32 = sb("X32", [128, 8, 256], F32, stack)
        H32 = sb("H32", [128, 8, 256], F32, stack)
        sq = sb("sq", [128, 8, 256], BF16, stack)
        rs = sb("rs", [128, 256], F32, stack)
    ps = [ES.enter_context(nc.psum_tensor("ps%d" % i, [128, 512], F32)) for i in range(6)]
    pb = [ES.enter_context(nc.psum_tensor("pb%d" % i, [128, 1024], BF16)) for i in range(2)]
    psi = [0]

    def nps():
        i = psi[0] % 6
        psi[0] += 1
        return ps[i], "ps%d" % i

    vei = [0]

    def ve():
        vei[0] += 1
        return "dve" if vei[0] % 2 else "pool"

    def mmg(out, pairs, r, w):
        def fn(e):
            n = len(pairs)
            ins = None
            for i, (l, rr) in enumerate(pairs):
                ins = e.matmul(out, lhsT=l, rhs=rr, start=(i == 0), stop=(i == n - 1))
            return ins
        P.op("pe", fn, r, w)

    def mms(outs_pairs, r, w):
        def fn(e):
            ins = None
            for (o, l, rr) in outs_pairs:
                ins = e.matmul(o, lhsT=l, rhs=rr, start=True, stop=True)
            return ins
        P.op("pe", fn, r, w)

    def trs(outs_ins, r, w):
        def fn(e):
            ins = None
            for (o, i) in outs_ins:
                ins = e.transpose(out=o, in_=i, identity=identb[:])
            return ins
        P.op("pe", fn, list(r) + ["c_identb"], w)

    def ACT(out, in_, func, r, w, **kw):
        P.op("act", lambda e: e.activation(out=out, in_=in_, func=func, **kw), r, w)

    def TT(eng, out, in0, in1, op, r, w):
        P.op(eng, lambda e: e.tensor_tensor(out=out, in0=in0, in1=in1, op=op), r, w)

    def TS(eng, out, in0, s1, s2, op0, op1, r, w):
        if s2 is None:
            P.op(eng, lambda e: e.tensor_scalar(out=out, in0=in0, scalar1=s1, scalar2=None, op0=op0), r, w)
        else:
            P.op(eng, lambda e: e.tensor_scalar(out=out, in0=in0, scalar1=s1, scalar2=s2, op0=op0, op1=op1), r, w)

    def STT(eng, out, in0, scalar, in1, op0, op1, r, w):
        P.op("dve", lambda e: e.scalar_tensor_tensor(out=out, in0=in0, scalar=scalar, in1=in1, op0=op0, op1=op1), r, w)

    def CP(eng, out, in_, r, w):
        if eng == "act":
            P.op("act", lambda e: e.copy(out=out, in_=in_), r, w)
        else:
            P.op(eng, lambda e: e.tensor_copy(out=out, in_=in_), r, w)

    def MEMSET(eng, ap, val, w):
        P.op(eng, lambda e: e.memset(ap, val), (), w)

    def RED(out, in_, r, w):
        P.op("dve", lambda e: e.tensor_reduce(out=out, in_=in_, axis=AX.X, op=ALU.add), r, w)

    def RECIP(out, in_, r, w):
        P.op("dve", lambda e: e.reciprocal(out=out, in_=in_), r, w)

    wst = [sb("wst%d" % i, [128, 1024], F32) for i in range(2)]
    wsti = [0]

    def flat(ap):
        nd = len(ap.shape)
        if nd == 2:
            return ap
        if nd == 3:
            return ap.rearrange("p a b -> p (a b)")
        return ap.rearrange("p a b c -> p (a b c)")

    def load_cast(dst, src, w, r=()):
        d2, s2 = flat(dst), flat(src)
        n = d2.shape[1]
        for c in range(0, n, 1024):
            m = min(1024, n - c)
            j = wsti[0] % 2
            wsti[0] += 1
            P.dma(wst[j][:, 0:m], s2[:, c:c + m], w=["wst%d" % j])
            CP("pool", d2[:, c:c + m], wst[j][:, 0:m], ["wst%d" % j] + list(r), w)

    def bc(ap, shape):
        return ap.broadcast_to(list(shape))

    for k in cs:
        if k in _skip:
            continue
        if CONST_SHAPES[k][1] == BF16:
            load_cast(cs[k][:], cd[k], [k])

    def norm_to_h(gcol, hbuf, hkey, c0):
        ACT(sq[:], X32[:], AF.Square, ["X32"], ["sq"])
        pt, pk = nps()
        mmg(pt[:, 0:256], [(onesdiv[:], sq[:, k, :]) for k in range(8)], ["sq", "c_onesdiv"], [pk])
        ACT(rs[:], pt[:, 0:256], AF.Sqrt, [pk], ["rs"], bias=EPS, scale=1.0)
        RECIP(rs[:], rs[:], ["rs"], ["rs"])
        for k in range(8):
            STT(ve(), hbuf[:, k, c0:c0 + 256], X32[:, k, :], gcol[:, k:k + 1], rs[:], ALU.mult, ALU.mult,
                ["X32", "rs", "gam"], [(hkey, k, c0 // 256)])

    def post_resid(gpost, blk, hnext):
        c0 = blk * 256
        ACT(sq[:], H32[:], AF.Square, ["H32"], ["sq"])
        pt, pk = nps()
        mmg(pt[:, 0:256], [(onesdiv[:], sq[:, k, :]) for k in range(8)], ["sq", "c_onesdiv"], [pk])
        ACT(rs[:], pt[:, 0:256], AF.Sqrt, [pk], ["rs"], bias=EPS, scale=1.0)
        RECIP(rs[:], rs[:], ["rs"], ["rs"])
        TT("dve", H32[:], H32[:], bc(rs[:].unsqueeze(1), [128, 8, 256]), ALU.mult, ["H32", "rs"], ["H32"])
        for k in range(8):
            STT(ve(), X32[:, k, :], H32[:, k, :], gpost[:, k:k + 1], X32[:, k, :], ALU.mult, ALU.add,
                ["H32", "X32", "gam"], ["X32"])
        P.dma(yT[:, :, c0:c0 + 256], X32[:], r=["X32"], w=[("yT", blk)])
        if hnext is not None:
            norm_to_h(hnext[0], hnext[1], hnext[2], c0)

    with ExitStack() as W0:
        alloc_norm(W0)
        for blk in range(NB):
            P.dma(X32[:], xT[:, :, blk * 256:(blk + 1) * 256], w=["X32"])
            norm_to_h(gam[:, 0, 0, :], bufA, "A", blk * 256)
        P.barrier()
    AK = lambda k, c0, n: [("A", k, b) for b in range(c0 // 256, (c0 + n + 255) // 256)]
    BK = lambda k, c0, n: [("B", k, b) for b in range(c0 // 256, (c0 + n + 255) // 256)]

    class _Stop(Exception):
        pass

    def chk(name):
        if stop == name:
            raise _Stop()

    try:
        chk("P0")
        for l in range(L):
            P.barrier()
            P.dma(dncv[:], dnconv_d[l], w=["dncv"])
            P.dma(fcv[:], fconv_d[l], w=["fcv"])
            P.dma(fbs[:], fbias_d[l], w=["fbs"])
            P.dma(dnr[:], dnorm_d[l], w=["dnr"])
            P.dma(alg[:], alog_d[l], w=["alg"])
            P.dma(dtbt[:], dtb_d[l], w=["dtbt"])
            ACT(negA[:], alg[:], AF.Exp, ["alg"], ["negA"])
            TS("dve", negA[:], negA[:], -1.0, None, ALU.mult, None, ["negA"], ["negA"])
            xsrc = xT if l == 0 else yT

            with ExitStack() as W:
                wcb = [sb("wcb%d" % i, [128, 8, 128], BF16, W) for i in range(2)]
                NU = 4
                U = [sb("U%d" % i, [128, 515], F32, W) for i in range(NU)]
                Y = [sb("Y%d" % i, [128, 512], F32, W) for i in range(NU)]
                NYB = 6
                Yb = [sb("Yb%d" % i, [128, 512], BF16, W) for i in range(NYB)]
                ucnt = 0
                Cs = sb("Cs", [128, 512], F32, W)
                Ss = sb("Ss", [128, 512], F32, W)
                xb = sb("xb", [128, 512], BF16, W)
                t1 = sb("t1", [128, 512], F32, W)
                t2 = sb("t2", [128, 512], F32, W)
                wvb = sb("wvb", [128, 8, 512], BF16, W)
                wabb = sb("wabb", [128, 8, 8], BF16, W)
                ybi = 0
                for ci in range(24):
                    if ci == 1:
                        chk("M1a")
                    if ci == 2:
                        chk("M1b")
                    if ci == 12:
                        chk("M1q")
                    if ci == 16:
                        chk("M1y")
                    if ci == 17:
                        chk("M1s")
                    if ci == 18:
                        chk("M1s2")
                    if ci == 20:
                        chk("M1t")
                    if ci == 13:
                        chk("M1z")
                    wt = wcb[ci % 2]
                    wk = "wcb%d" % (ci % 2)
                    load_cast(wt[:], win_c[l, ci], [wk])
                    for tb in range(NTB):
                        c0 = tb * 512
                        pt, pk = nps()
                        mmg(pt[:], [(wt[:, k, :], bufA[:, k, c0:c0 + 512]) for k in range(8)],
                            [wk] + sum([AK(k, c0, 512) for k in range(8)], []), [pk])
                        yb = Yb[ybi % NYB]
                        ybk = "Yb%d" % (ybi % NYB)
                        ybi += 1
                        if ci < 12:
                            Uc, Un = U[ucnt % NU], U[(ucnt + 1) % NU]
                            uck, unk = "U%d" % (ucnt % NU), "U%d" % ((ucnt + 1) % NU)
                            if tb == 0:
                                MEMSET("pool", Uc[:, 0:3], 0.0, [uck + "h"])
                            CP("act", Uc[:, 3:515], pt[:], [pk], [uck])
                            if tb < NTB - 1:
                                CP("pool", Un[:, 0:3], Uc[:, 512:515], [uck], [unk + "h"])
                            yy = Y[ucnt % NU]
                            yk = "Y%d" % (ucnt % NU)
                            ucnt += 1
                            e1 = ve()
                            TS(e1, yy[:], Uc[:, 0:512], dncv[:, ci, 0:1], None, ALU.mult, None, [uck, uck + "h", "dncv"], [yk])
                            for j in range(1, 4):
                                STT(e1, yy[:], Uc[:, j:j + 512], dncv[:, ci, j:j + 1], yy[:], ALU.mult, ALU.add,
                                    [uck, uck + "h", "dncv", yk], [yk])
                            ACT(yb[:], yy[:], AF.Silu, [yk], [ybk])
                            P.dma(qkvT[ci, :, c0:c0 + 512], yb[:], r=[ybk], w=[("qkvT", ci, tb)])
                        elif ci < 16:
                            ACT(yb[:], pt[:], AF.Silu, [pk], [ybk])
                            P.dma(zT[ci - 12, :, c0:c0 + 512], yb[:], r=[ybk], w=[("zT", ci - 12, tb)])
                        else:
                            P.dma(Cs[:], ropeC_d[:, c0:c0 + 512], w=["Cs"])
                            P.dma(Ss[:], ropeS_d[:, c0:c0 + 512], w=["Ss"])
                            CP("act", xb[:], pt[:], [pk], ["xb"])
                            TT("dve", t1[:], pt[:], Cs[:], ALU.mult, [pk, "Cs", "xb"], ["t1"])
                            p2, p2k = nps()
                            mmg(p2[:], [(pm[:], xb[:])], ["xb", "c_pm"], [p2k])
                            TT("dve", t2[:], p2[:], Ss[:], ALU.mult, [p2k, "Ss"], ["t2"])
                            TT("pool", yb[:], t1[:], t2[:], ALU.add, ["t1", "t2"], [ybk])
                            if ci < 20:
                                P.dma(qrT[ci - 16, :, c0:c0 + 512], yb[:], r=[ybk], w=[("qrT", ci - 16, tb)])
                            else:
                                P.dma(krT[ci - 20, :, c0:c0 + 512], yb[:], r=[ybk], w=[("krT", ci - 20, tb)])
                chk("M1r")
                load_cast(wvb[:], wv_d[l], ["wvb"])
                for t in range(NT):
                    c0 = t * 128
                    pt, pk = nps()
                    mmg(pt[:], [(bufA[:, k, c0:c0 + 128], wvb[:, k, :]) for k in range(8)],
                        ["wvb"] + sum([AK(k, c0, 128) for k in range(8)], []), [pk])
                    yb = Yb[ybi % NYB]
                    ybk = "Yb%d" % (ybi % NYB)
                    ybi += 1
                    CP("act" if t % 2 else "dve", yb[:], pt[:], [pk], [ybk])
                    P.dma(vtok[c0:c0 + 128, :], yb[:], r=[ybk], w=[("vtok", t)])
                chk("M1v")
                load_cast(wabb[:], wab_d[l], ["wabb"])
                pt, pk = nps()
                for t in range(NT):
                    c0 = t * 128
                    mmg(pt[:, t * 8:(t + 1) * 8], [(bufA[:, k, c0:c0 + 128], wabb[:, k, :]) for k in range(8)],
                        ["wabb"] + sum([AK(k, c0, 128) for k in range(8)], []), [pk])
                CP("act", gbraw[:].rearrange("p t c -> p (t c)"), pt[:, 0:NT * 8], [pk], ["gbraw"])
                ACT(beta[:], gbraw[:, :, 0:4], AF.Sigmoid, ["gbraw"], ["beta"])
                TT("dve", tmp4[:], gbraw[:, :, 4:8], bc(dtbt[:].unsqueeze(1), [128, NT, 4]), ALU.add, ["gbraw", "dtbt"], ["tmp4"])
                ACT(tmp4[:], tmp4[:], AF.Exp, ["tmp4"], ["tmp4"])
                ACT(tmp4[:], tmp4[:], AF.Ln, ["tmp4"], ["tmp4"], bias=1.0, scale=1.0)
                TT("dve", gg[:], tmp4[:], bc(negA[:].unsqueeze(1), [128, NT, 4]), ALU.mult, ["tmp4", "negA"], ["gg"])
                pt, pk = nps()
                ggf = gg[:].rearrange("p t c -> p (t c)")
                mmg(pt[:, 0:NT * 4], [(triu32[:], ggf)], ["gg", "c_triu32"], [pk])
                CP("act", Gc[:].rearrange("p t c -> p (t c)"), pt[:, 0:NT * 4], [pk], ["Gc"])
                pt, pk = nps()
                mmg(pt[:, 0:NT * 4], [(ones32[:], ggf)], ["gg", "c_ones32"], [pk])
                CP("act", Gl[:].rearrange("p t c -> p (t c)"), pt[:, 0:NT * 4], [pk], ["Gl"])
                ACT(eG[:], Gc[:], AF.Exp, ["Gc"], ["eG"])
                ACT(dcc[:], Gl[:], AF.Exp, ["Gl"], ["dcc"])
                TT("dve", eGL[:], Gl[:], Gc[:], ALU.subtract, ["Gl", "Gc"], ["eGL"])
                ACT(eGL[:], eGL[:], AF.Exp, ["eGL"], ["eGL"])
                if dbg and l == 0:
                    P.dma(gdbg[:, 0, :], gg[:].rearrange("p t c -> p (t c)"), r=["gg"], w=["gdbg0"])
                    P.dma(gdbg[:, 1, :], beta[:].rearrange("p t c -> p (t c)"), r=["beta"], w=["gdbg1"])
                    P.dma(gdbg[:, 2, :], Gc[:].rearrange("p t c -> p (t c)"), r=["Gc"], w=["gdbg2"])
            P.barrier()
            chk("M1")

            with ExitStack() as W:
                qkvg = [sb("qkvg%d" % i, [128, 12, 512], BF16, W) for i in range(2)]
                zg = [sb("zg%d" % i, [128, 4, 512], BF16, W) for i in range(2)]
                sqt = sb("sqt", [128, 8, 128], F32, W)
                ssq = sb("ssq", [128, 8], F32, W)
                rn = sb("rn", [128, 8], F32, W)
                sck = sb("sck", [128, 4, 4], F32, W)
                scq = sb("scq", [128, 2, 4], F32, W)
                ktok4 = sb("ktok4", [128, 4, 4, 128], BF16, W)
                qtok2 = sb("qtok2", [128, 2, 4, 128], BF16, W)
                bv = sb("bv", [128, 4, 128], BF16, W)
                fT = sb("fT", [128, 4, 4, 128], BF16, W)
                TG = sb("TG", [128, 4, 128], F32, W)
                Dp = sb("Dp", [128, 4, 128], F32, W)
                gm = sb("gm", [128, 2, 4, 128], F32, W)
                AQ = sb("AQ", [128, 2, 4, 128], BF16, W)
                AQT = sb("AQT", [128, 2, 4, 128], BF16, W)
                D2 = sb("D2", [128, 2, 4, 128], BF16, W)
                tmpD = sb("tmpD", [128, 2, 4, 128], F32, W)
                Xb = sb("Xb", [128, 4, 128], BF16, W)
                WTb = sb("WTb", [128, 4, 128], BF16, W)
                U32 = sb("U32", [128, 4, 128], F32, W)
                vn = sb("vn", [128, 4, 128], BF16, W)
                osq = sb("osq", [128, 4, 128], F32, W)
                oss = sb("oss", [128, 4], F32, W)
                onb = sb("onb", [128, 4, 128], BF16, W)
                MEMSET("pool", Sst[:], 0.0, ["Sst"])
                MEMSET("pool", Sb_[:], 0.0, ["Sb"])
                for t in range(NT):
                    g4, tt = t // 4, t % 4
                    qg_, zg_ = qkvg[g4 % 2], zg[g4 % 2]
                    qgk, zgk = "qkvg%d" % (g4 % 2), "zg%d" % (g4 % 2)
                    if tt == 0:
                        c0 = g4 * 512
                        P.dma(qg_[:], qkvT[:, :, c0:c0 + 512].rearrange("c p n -> p c n"), w=[qgk])
                        P.dma(zg_[:], zT[:, :, c0:c0 + 512].rearrange("c p n -> p c n"), w=[zgk])
                    cc = slice(tt * 128, (tt + 1) * 128)
                    pbA = pb[0][:].rearrange("p (c n) -> p c n", n=128)
                    pbB = pb[1][:].rearrange("p (c n) -> p c n", n=128)
                    trs([(pbA[:, c, :], qg_[:, c, cc]) for c in range(8)], [qgk], ["pb0"])
                    trs([(pbB[:, c, :], qg_[:, 8 + c, cc]) for c in range(4)], [qgk], ["pb1"])
                    ACT(sqt[:], pbA, AF.Square, ["pb0"], ["sqt"])
                    RED(ssq[:], sqt[:], ["sqt"], ["ssq"])
                    ACT(rn[:], ssq[:], AF.Sqrt, ["ssq"], ["rn"], bias=EPS, scale=1.0)
                    RECIP(rn[:], rn[:], ["rn"], ["rn"])
                    bt, egt, eglt = beta[:, t, :], eG[:, t, :], eGL[:, t, :]
                    CP("pool", sck[:, 2, :], rn[:, 4:8], ["rn"], ["sck2"])
                    TT("dve", sck[:, 3, :], rn[:, 4:8], bt, ALU.mult, ["rn", "beta"], ["sck3"])
                    TT("dve", sck[:, 1, :], sck[:, 3, :], egt, ALU.mult, ["sck3", "eG"], ["sck1"])
                    TT("dve", sck[:, 0, :], rn[:, 4:8], eglt, ALU.mult, ["rn", "eGL"], ["sck0"])
                    TS("dve", scq[:, 0, :], rn[:, 0:4], float(128 ** -0.5), None, ALU.mult, None, ["rn"], ["scq0"])
                    TT("dve", scq[:, 1, :], scq[:, 0, :], egt, ALU.mult, ["scq0", "eG"], ["scq1"])
                    SK = ["sck0", "sck1", "sck2", "sck3"]
                    TT("dve", ktok4[:], bc(pbA[:, 4:8, :].unsqueeze(1), [128, 4, 4, 128]),
                       bc(sck[:].unsqueeze(3), [128, 4, 4, 128]), ALU.mult, ["pb0"] + SK, ["ktok4"])
                    TT("dve", qtok2[:], bc(pbA[:, 0:4, :].unsqueeze(1), [128, 2, 4, 128]),
                       bc(scq[:].unsqueeze(3), [128, 2, 4, 128]), ALU.mult, ["pb0", "scq0", "scq1"], ["qtok2"])
                    TT("dve", bv[:], pbB[:, 0:4, :], bc(bt.unsqueeze(2), [128, 4, 128]), ALU.mult, ["pb1", "beta"], ["bv"])
                    trs([(pbA[:, c, :], ktok4[:, 2, c, :]) for c in range(4)] +
                        [(pbA[:, 4 + c, :], ktok4[:, 3, c, :]) for c in range(4)], ["ktok4"], ["pb0"])
                    trs([(pbB[:, c, :], qtok2[:, 0, c, :]) for c in range(4)] +
                        [(pbB[:, 4 + c, :], qtok2[:, 1, c, :]) for c in range(4)], ["qtok2"], ["pb1"])
                    CP("act", fT[:, 0:2, :, :].rearrange("p a h n -> p (a h) n"), pbA, ["pb0"], ["fT01"])
                    CP("dve", fT[:, 2:4, :, :].rearrange("p a h n -> p (a h) n"), pbB, ["pb1"], ["fT23"])
                    pK, pKk = nps()
                    pQ, pQk = nps()
                    pK3 = pK[:].rearrange("p (h n) -> p h n", n=128)
                    pQ3 = pQ[:].rearrange("p (h n) -> p h n", n=128)
                    mms([(pK3[:, h, :], fT[:, 1, h, :], fT[:, 0, h, :]) for h in range(4)], ["fT01"], [pKk])
                    mms([(pQ3[:, h, :], fT[:, 2, h, :], fT[:, 0, h, :]) for h in range(4)], ["fT01", "fT23"], [pQk])
                    TT("pool", TG[:], bc(triu32[:].unsqueeze(1), [128, 4, 128]), bc(gg[:, t, :].unsqueeze(2), [128, 4, 128]),
                       ALU.mult, ["gg", "c_triu32"], ["TG"])
                    pG, pGk = nps()
                    pG3 = pG[:].rearrange("p (h n) -> p h n", n=128)
                    mms([(pG3[:, h, :], ones32[:], TG[:, h, :]) for h in range(4)], ["TG", "c_ones32"], [pGk])
                    TT("dve", Dp[:], pG3, bc(Gc[:, t, :].unsqueeze(2), [128, 4, 128]), ALU.subtract, [pGk, "Gc"], ["Dp"])
                    TT("pool", Dp[:], Dp[:], bc(mask2[:, 1, :].unsqueeze(1), [128, 4, 128]), ALU.mult, ["Dp", "c_mask2"], ["Dp"])
                    ACT(Dp[:], Dp[:], AF.Exp, ["Dp"], ["Dp"], scale=-1.0)
                    TT("pool", gm[:], bc(Dp[:].unsqueeze(1), [128, 2, 4, 128]), bc(mask2[:].unsqueeze(2), [128, 2, 4, 128]),
                       ALU.mult, ["Dp", "c_mask2"], ["gm"])
                    TT("dve", AQ[:, 0], pK3, gm[:, 0], ALU.mult, [pKk, "gm"], ["AQ0"])
                    TT("dve", AQ[:, 1], pQ3, gm[:, 1], ALU.mult, [pQk, "gm"], ["AQ1"])
                    trs([(pbA[:, c, :], AQ[:, 0, c, :]) for c in range(4)] +
                        [(pbA[:, 4 + c, :], AQ[:, 1, c, :]) for c in range(4)], ["AQ0", "AQ1"], ["pb0"])
                    CP("act", AQT[:].rearrange("p a h n -> p (a h) n"), pbA, ["pb0"], ["AQT"])
                    TT("pool", tmpD[:, 0], AQ[:, 0], bc(negm[:, 0, 0, :].unsqueeze(1), [128, 4, 128]), ALU.mult, ["AQ0", "c_negm"], ["tmpD0"])
                    TT("pool", tmpD[:, 1], AQT[:, 0], bc(negm[:, 0, 1, :].unsqueeze(1), [128, 4, 128]), ALU.mult, ["AQT", "c_negm"], ["tmpD1"])
                    TT("dve", D2[:], tmpD[:], bc(ident2[:].unsqueeze(2), [128, 2, 4, 128]), ALU.add, ["tmpD0", "tmpD1", "c_ident2"], ["D2"])
                    for lv in range(1, 7):
                        pX, pXk = nps()
                        pX3 = pX[:].rearrange("p (h n) -> p h n", n=128)
                        mms([(pX3[:, h, :], AQT[:, 0, h, :], D2[:, 0, h, :]) for h in range(4)], ["AQT", "D2"], [pXk])
                        CP("act", Xb[:], pX3, [pXk], ["Xb"])
                        pY, pYk = nps()
                        pZ, pZk = nps()
                        pY3 = pY[:].rearrange("p (h n) -> p h n", n=128)
                        pZ3 = pZ[:].rearrange("p (h n) -> p h n", n=128)
                        mms([(pY3[:, h, :], D2[:, 1, h, :], Xb[:, h, :]) for h in range(4)], ["D2", "Xb"], [pYk])
                        mms([(pZ3[:, h, :], Xb[:, h, :], D2[:, 1, h, :]) for h in range(4)], ["D2", "Xb"], [pZk])
                        TT("dve", tmpD[:, 0], pY3, bc(negm[:, lv, 0, :].unsqueeze(1), [128, 4, 128]), ALU.mult, [pYk, "c_negm"], ["tmpD0"])
                        TT("dve", tmpD[:, 1], pZ3, bc(negm[:, lv, 1, :].unsqueeze(1), [128, 4, 128]), ALU.mult, [pZk, "c_negm"], ["tmpD1"])
                        TT("pool", D2[:], D2[:], tmpD[:], ALU.add, ["tmpD0", "tmpD1", "D2"], ["D2"])
                    pW, pWk = nps()
                    pU, pUk = nps()
                    pW3 = pW[:].rearrange("p (h n) -> p h n", n=128)
                    pU3 = pU[:].rearrange("p (h n) -> p h n", n=128)
                    mms([(pW3[:, h, :], ktok4[:, 1, h, :], D2[:, 1, h, :]) for h in range(4)], ["ktok4", "D2"], [pWk])
                    mms([(pU3[:, h, :], D2[:, 1, h, :], bv[:, h, :]) for h in range(4)], ["bv", "D2"], [pUk])
                    CP("act", WTb[:], pW3, [pWk], ["WTb"])
                    CP("dve", U32[:], pU3, [pUk], ["U32"])
                    pS, pSk = nps()
                    pS3 = pS[:].rearrange("p (h n) -> p h n", n=128)
                    mms([(pS3[:, h, :], WTb[:, h, :], Sb_[:, h, :]) for h in range(4)], ["WTb", "Sb"], [pSk])
                    TT("dve", vn[:], U32[:], pS3, ALU.subtract, ["U32", pSk], ["vn"])
                    pO, pOk = nps()
                    pO3 = pO[:].rearrange("p (h n) -> p h n", n=128)

                    def ofn(e, pO3=pO3, fT=fT, AQT=AQT, vn=vn):
                        ins = None
                        for h in range(4):
                            e.matmul(pO3[:, h, :], lhsT=fT[:, 3, h, :], rhs=Sb_[:, h, :], start=True, stop=False)
                            ins = e.matmul(pO3[:, h, :], lhsT=AQT[:, 1, h, :], rhs=vn[:, h, :], start=False, stop=True)
                        return ins
                    P.op("pe", ofn, ["fT23", "Sb", "AQT", "vn"], [pOk])
                    pN, pNk = nps()
                    pN3 = pN[:].rearrange("p (h n) -> p h n", n=128)
                    mms([(pN3[:, h, :], ktok4[:, 0, h, :], vn[:, h, :]) for h in range(4)], ["ktok4", "vn"], [pNk])
                    TT("pool", Sst[:], Sst[:], bc(dcc[:, t, :].unsqueeze(2), [128, 4, 128]), ALU.mult, ["Sst", "dcc"], ["Sst"])
                    TT("dve", Sst[:], Sst[:], pN3, ALU.add, ["Sst", pNk], ["Sst"])
                    CP("act", Sb_[:], Sst[:], ["Sst"], ["Sb"])
                    ACT(osq[:], pO3, AF.Square, [pOk], ["osq"])
                    RED(oss[:], osq[:], ["osq"], ["oss"])
                    ACT(oss[:], oss[:], AF.Sqrt, ["oss"], ["oss"], bias=EPS, scale=1.0 / 128)
                    RECIP(oss[:], oss[:], ["oss"], ["oss"])
                    TT("dve", onb[:], pO3, bc(oss[:].unsqueeze(2), [128, 4, 128]), ALU.mult, [pOk, "oss"], ["onb"])
                    trs([(pbB[:, c, :], onb[:, c, :]) for c in range(4)], ["onb"], ["pb1"])
                    STT("dve", bufA[:, 0:4, t * 128:(t + 1) * 128], pbB[:, 0:4, :], dnr[:, 0:1], zg_[:, :, cc], ALU.mult, ALU.mult,
                        ["pb1", "dnr", zgk], [("A", k, t // 2) for k in range(4)])
            P.barrier()
            chk("M2")

            with ExitStack() as W:
                qh1 = sb("qh", [128, S], BF16, W)
                kz = [sb("kz%d" % i, [128, S], BF16, W) for i in range(2)]
                MEMSET("pool", kz[0][64:128, :], 0.0, ["kz0z"])
                MEMSET("pool", kz[1][0:64, :], 0.0, ["kz1z"])
                Oacc = sb("Oacc", [128, 2, S], F32, W)
                qdd = sb("qdd", [128, S], BF16, W)
                kdd = [sb("kdd%d" % i, [128, S], BF16, W) for i in range(2)]
                NV = 4
                Vt = [sb("Vt%d" % i, [128, 4, 64], BF16, W) for i in range(NV)]
                E = [sb("E%d" % i, [128, 2, 2, 128], BF16, W) for i in range(2)]
                Pt = [sb("Pt%d" % i, [128, 2, 2, 128], BF16, W) for i in range(2)]
                rden = sb("rden", [128, 512], F32, W)
                for i in range(NV):
                    MEMSET("pool", Vt[i][:, 1:3, :], 1.0, ["Vt%d" % i])
                vi = 0
                it = 0
                for hp in range(4):
                    qt, qk_ = qh1, "qh"
                    P.dma(qt[:], qrT[hp], w=[qk_])
                    P.dma(kz[0][0:64, :], krT[hp, 0:64, :], w=["kz0"])
                    P.dma(kz[1][64:128, :], krT[hp, 64:128, :], w=["kz1"])
                    for ci_, dil in enumerate((1, 4, 16)):
                        Lr = S // dil
                        nb = Lr // 128
                        if dil == 1:
                            qsrc, ksrc, qsk, ksk = qt, kz, qk_, ["kz0", "kz1", "kz0z", "kz1z"]
                        else:
                            CP("pool", qdd[:].rearrange("p (r i) -> p r i", r=dil), qt[:].rearrange("p (i r) -> p r i", r=dil), [qk_], ["qdd"])
                            CP("act", kdd[0][:].rearrange("p (r i) -> p r i", r=dil), kz[0][:].rearrange("p (i r) -> p r i", r=dil), ["kz0", "kz0z"], ["kdd0"])
                            CP("pool", kdd[1][:].rearrange("p (r i) -> p r i", r=dil), kz[1][:].rearrange("p (i r) -> p r i", r=dil), ["kz1", "kz1z"], ["kdd1"])
                            qsrc, ksrc, qsk, ksk = qdd, kdd, "qdd", ["kdd0", "kdd1"]
                        for r in range(dil):
                            prev = None
                            for b in range(nb):
                                st_ = 128 * b * dil + r
                                sl = slice(st_, st_ + 127 * dil + 1, dil)
                                vt = Vt[vi % NV]
                                vk = "Vt%d" % (vi % NV)
                                vi += 1
                                P.dma(vt[:, 0:4:3, :], vtok[sl, hp * 128:(hp + 1) * 128].rearrange("n (h d) -> n h d", h=2),
                                      w=[vk + "v"], r=[vk])
                                pt, pk = nps()
                                p4 = pt[:].rearrange("p (h k n) -> p h k n", h=2, k=2)
                                cl = slice(r * Lr + 128 * b, r * Lr + 128 * b + 128)
                                kbs = ([(0, prev)] if prev is not None else []) + [(1, (cl, vt, vk))]
                                mms([(p4[:, h, kb, :], ksrc[h][:, blk[0]], qsrc[:, cl])
                                     for h in range(NH_) for (kb, blk) in kbs], [qsk] + ksk, [pk])
                                e_, ek = E[it % 2], "E%d" % (it % 2)
                                p_, pk2 = Pt[it % 2], "Pt%d" % (it % 2)
                                it += 1
                                k0 = 0 if prev is not None else 1
                                ACT(e_[:, :, k0:2, :], p4[:, :, k0:2, :], AF.Exp, [pk], [ek], scale=0.125)
                                TT("pool", p_[:, :, k0:2, :], e_[:, :, k0:2, :], bc(maska[:, k0:2, :].unsqueeze(1), [128, 2, 2 - k0, 128]),
                                   ALU.mult, [ek, "c_maska"], [pk2])
                                po, pok = nps()
                                po3 = po[:, 0:256].rearrange("p (h n) -> p h n", h=2)

                                def pvfn(e, po3=po3, kbs=kbs, p_=p_):
                                    ins = None
                                    for h in range(NH_):
                                        for i, (kb, blk) in enumerate(kbs):
                                            ins = e.matmul(po3[:, h, :], lhsT=blk[1][:, 2 * h:2 * h + 2, :].rearrange("p a d -> p (a d)"),
                                                           rhs=p_[:, h, kb, :], start=(i == 0), stop=(i == len(kbs) - 1))
                                    return ins
                                P.op("pe", pvfn, [pk2] + [blk[2] + "v" for (_, blk) in kbs] + [blk[2] for (_, blk) in kbs], [pok])
                                if ci_ == 0:
                                    CP("dve", Oacc[:, :, sl], po3, [pok], [("Oacc", b // 2)])
                                else:
                                    okeys = [("Oacc", x) for x in range(st_ // 256, (st_ + 128 * dil + 255) // 256)]
                                    TT("dve", Oacc[:, :, sl], Oacc[:, :, sl], po3, ALU.add, [pok] + okeys, okeys)
                                prev = (cl, vt, vk)
                                if hp == 0 and ci_ == 0 and b == 0:
                                    chk("M3a")
                                if hp == 0 and ci_ == 0 and b == 1:
                                    chk("M3b")
                        if hp == 0:
                            chk("M3c%d" % ci_)
                    for tb in range(NTB):
                        c0 = tb * 512
                        for h in range(2):
                            pd, pdk = nps()
                            mmg(pd[:], [(shiftm[:, h, :], Oacc[:, h, c0:c0 + 512])],
                                ["c_shift", ("Oacc", 2 * tb), ("Oacc", 2 * tb + 1)], [pdk])
                            hs = slice(h * 64, (h + 1) * 64)
                            RECIP(rden[hs, :], pd[hs, :], [pdk], ["rden%d" % h])
                            TT("pool", bufA[hs, 4 + hp, c0:c0 + 512], Oacc[hs, h, c0:c0 + 512], rden[hs, :], ALU.mult,
                               ["rden%d" % h, ("Oacc", 2 * tb), ("Oacc", 2 * tb + 1)], [("A", 4 + hp, 2 * tb, h), ("A", 4 + hp, 2 * tb + 1, h)])
                    if hp == 0:
                        chk("M3f")
                if dbg and l == 0:
                    P.barrier()
                    P.dma(mixdbg, bufA[:], w=["mixdbg"])
            P.barrier()
            chk("M3")

            with ExitStack() as WB:
                bufB = sb("bufB", [128, 8, S], BF16, WB)
                with ExitStack() as W:
                    woutb = sb("woutb", [128, 8, 1024], BF16, W)
                    alloc_norm(W)
                    for k in range(8):
                        load_cast(woutb[:, k, :], wout_d[l, :, k, :], ["woutb%d" % k])
                    for blk in range(NB):
                        c0 = blk * 256
                        for c in range(8):
                            pt, pk = nps()
                            mmg(pt[:, 0:256], [(woutb[:, k, c * 128:(c + 1) * 128], bufA[:, k, c0:c0 + 256]) for k in range(8)],
                                ["woutb%d" % k for k in range(8)], [pk])
                            CP("act" if c % 2 else "dve", H32[:, c, :], pt[:, 0:256], [pk], ["H32"])
                        P.dma(X32[:], xsrc[:, :, c0:c0 + 256], w=["X32"])
                        post_resid(gam[:, l, 1, :], blk, (gam[:, l, 2, :], bufB, "B"))
                P.barrier()
                with ExitStack() as W:
                    wjb = [sb("wjb%d" % i, [128, 8, 2, 128], BF16, W) for i in range(2)]
                    NF = 3
                    Ug = [sb("Ug%d" % i, [128, 514], F32, W) for i in range(NF)]
                    Uv = [sb("Uv%d" % i, [128, 514], F32, W) for i in range(NF)]
                    Yg = [sb("Yg%d" % i, [128, 512], F32, W) for i in range(NF)]
                    Yv = [sb("Yv%d" % i, [128, 512], F32, W) for i in range(NF)]
                    g1s = [sb("g1_%d" % i, [128, 512], F32, W) for i in range(NF)]
                    g2s = [sb("g2_%d" % i, [128, 512], F32, W) for i in range(NF)]
                    ab = [sb("ab%d" % i, [128, 512], BF16, W) for i in range(NF)]
                    it = 0
                    for j in range(22):
                        wt, wk = wjb[j % 2], "wjb%d" % (j % 2)
                        load_cast(wt[:], wup_d[l, j], [wk])
                        for tb in range(NTB):
                            c0 = tb * 512
                            i2 = it % NF
                            i3 = (it + 1) % NF
                            it += 1
                            g1, g2 = g1s[i2], g2s[i2]
                            g1k, g2k = "g1_%d" % i2, "g2_%d" % i2
                            pg, pgk = nps()
                            pv, pvk = nps()
                            bkeys = sum([BK(k, c0, 512) for k in range(8)], [])
                            mmg(pg[:], [(wt[:, k, 0, :], bufB[:, k, c0:c0 + 512]) for k in range(8)], [wk] + bkeys, [pgk])
                            mmg(pv[:], [(wt[:, k, 1, :], bufB[:, k, c0:c0 + 512]) for k in range(8)], [wk] + bkeys, [pvk])
                            for (Ux, px, pxk, nm, cj, Yx) in ((Ug, pg, pgk, "Ug", j, Yg), (Uv, pv, pvk, "Uv", 22 + j, Yv)):
                                Uc, Un = Ux[i2], Ux[i3]
                                uck, unk = nm + str(i2), nm + str(i3)
                                if tb == 0:
                                    MEMSET("pool", Uc[:, 0:2], 0.0, [uck + "h"])
                                CP("act", Uc[:, 2:514], px[:], [pxk], [uck])
                                if tb < NTB - 1:
                                    CP("pool", Un[:, 0:2], Uc[:, 512:514], [uck], [unk + "h"])
                                yy, yk = Yx[i2], nm + "Y%d" % i2
                                en = "pool"
                                TS(en, yy[:], Uc[:, 0:512], fcv[:, cj, 0:1], fbs[:, cj:cj + 1], ALU.mult, ALU.add,
                                   [uck, uck + "h", "fcv", "fbs"], [yk])
                                for jj in range(1, 3):
                                    STT(en, yy[:], Uc[:, jj:jj + 512], fcv[:, cj, jj:jj + 1], yy[:], ALU.mult, ALU.add,
                                        [uck, uck + "h", "fcv", yk], [yk])
                            yg, yv = Yg[i2], Yv[i2]
                            ygk, yvk = "UgY%d" % i2, "UvY%d" % i2
                            ACT(g1[:], yg[:], AF.Square, [ygk], [g1k])
                            TS("pool", g1[:], g1[:], 0.044715, 1.0, ALU.mult, ALU.add, [g1k], [g1k])
                            TT("pool", g1[:], g1[:], yg[:], ALU.mult, [g1k, ygk], [g1k])
                            ACT(g1[:], g1[:], AF.Sigmoid, [g1k], [g1k], scale=1.5957691216057308)
                            TT("pool", g2[:], yg[:], yv[:], ALU.mult, [ygk, yvk], [g2k])
                            TT("dve", ab[i2][:], g2[:], g1[:], ALU.mult, [g1k, g2k], ["ab%d" % i2])
                            P.dma(aT[j, :, c0:c0 + 512], ab[i2][:], r=["ab%d" % i2], w=[("aT", j, tb)])
            P.barrier()
            chk("F1")

            with ExitStack() as W:
                wdb = sb("wdb", [128, 22, 1024], BF16, W)
                alloc_norm(W)
                ablk = [sb("ablk%d" % i, [128, 22, 256], BF16, W) for i in range(2)]
                for k in range(22):
                    load_cast(wdb[:, k, :], wdn_d[l, :, k, :], ["wdb%d" % k])
                for blk in range(NB):
                    c0 = blk * 256
                    at, atk = ablk[blk % 2], "ablk%d" % (blk % 2)
                    P.dma(at[:], aT[:, :, c0:c0 + 256].rearrange("c p n -> p c n"), w=[atk])
                    for c in range(8):
                        pt, pk = nps()
                        mmg(pt[:, 0:256], [(wdb[:, k, c * 128:(c + 1) * 128], at[:, k, :]) for k in range(22)],
                            [atk] + ["wdb%d" % k for k in range(22)], [pk])
                        CP("act" if c % 2 else "dve", H32[:, c, :], pt[:, 0:256], [pk], ["H32"])
                    P.dma(X32[:], yT[:, :, c0:c0 + 256], w=["X32"], r=[("yT", blk)])
                    nxt = (gam[:, l + 1, 0, :], bufA, "A") if l + 1 < L else None
                    post_resid(gam[:, l, 3, :], blk, nxt)

    except _Stop:
        pass
    P.barrier()
    P.emit()
    try:
        ES.close()
    except AssertionError:
        pass
    return nc, P


_CACHE = {}


def kernel(**inputs):
    x = np.asarray(inputs["x"], dtype=np.float32)
    B, S, D = x.shape
    L = DEPTH
    key = (S, L)
    if key not in _CACHE:
        _CACHE[key] = build_program(S, L)
    nc, _ = _CACHE[key]
    shared = layout_weights(inputs, L)
    shared.update(make_consts(S))
    in_maps = []
    for b in range(B):
        m = dict(shared)
        m["xT"] = np.ascontiguousarray(x[b].T.reshape(8, 128, S).transpose(1, 0, 2))
        in_maps.append(m)
    out = np.empty((B, S, D), np.float32)
    if MULTI_CORE:
        res = run_bass_kernel_spmd(nc, in_maps, core_ids=list(range(B)))
        results = res.results
    else:
        results = [run_bass_kernel_spmd(nc, [in_maps[b]], core_ids=[0]).results[0] for b in range(B)]
    for b in range(B):
        yT = np.asarray(results[b]["yT"], dtype=np.float32)
        out[b] = yT.transpose(1, 0, 2).reshape(D, S).T
    return out
```
